# Optimizing a Trainium2 kernel written in Bass

```python
import jax, jax.numpy as jnp
from jax import lax
import numpy as np

D_MODEL = 4096
BATCH = 2
SEQ = 8192
DEPTH = 1

HEAD_DIM = 128
HGRN_EXPAND = 128
HGRN_WIDTH = D_MODEL // 2
HGRN_HEADS = HGRN_WIDTH // HGRN_EXPAND
HGRN_VDIM = HGRN_WIDTH // HGRN_HEADS
ATTN_WIDTH = D_MODEL // 2
ATTN_HEADS = ATTN_WIDTH // HEAD_DIM
KV_HEADS = ATTN_HEADS // 4
GQA_GROUP = ATTN_HEADS // KV_HEADS
WINDOW = 128
BLOCK = 128
CHUNK = 64
D_FF = 4 * D_MODEL
PLE_DIM = 256
ROPE_THETA = 10000.0
EPS = 1e-6

kernel_name = "hybrid_hgrn2_swa_sink_encoder_layer"


def in_proj_sizes():
    return [HGRN_HEADS * HGRN_EXPAND, HGRN_HEADS * HGRN_EXPAND, HGRN_HEADS * HGRN_EXPAND,
            HGRN_WIDTH, HGRN_WIDTH,
            ATTN_HEADS * HEAD_DIM, KV_HEADS * HEAD_DIM, KV_HEADS * HEAD_DIM,
            D_MODEL, D_MODEL]


def rms_norm(x, w):
    xf = x.astype(jnp.float32)
    y = xf * lax.rsqrt(jnp.mean(xf * xf, axis=-1, keepdims=True) + EPS)
    return (y * w.astype(jnp.float32)).astype(x.dtype)


def layer_lower_bound(lb_param, layer):
    lb = jnp.cumsum(jax.nn.softmax(lb_param.astype(jnp.float32), axis=0), axis=0)
    return lb[layer]


def gla_chunkwise(q, k, v, g):
    B, S, H, K = q.shape
    V = v.shape[-1]
    N = S // CHUNK

    def to_chunks(t):
        return t.reshape(B, N, CHUNK, H, t.shape[-1]).transpose(1, 0, 3, 2, 4)

    q, k, v, g = map(to_chunks, (q, k, v, g))
    b = jnp.cumsum(g, axis=3)
    b_last = b[:, :, :, -1:, :]
    q_hat = q * jnp.exp(b)
    k_hat = k * jnp.exp(-b)
    k_tail = k * jnp.exp(b_last - b)
    causal_in_chunk = jnp.tril(jnp.ones((CHUNK, CHUNK), dtype=bool))
    a = jnp.einsum('nbhck,nbhsk->nbhcs', q_hat, k_hat)
    a = jnp.where(causal_in_chunk, a, 0.0)
    o_intra = jnp.einsum('nbhcs,nbhsv->nbhcv', a, v)
    decay = jnp.exp(b_last[:, :, :, 0, :])

    def step(state, xs):
        q_n, kt_n, v_n, d_n = xs
        o_n = jnp.einsum('bhck,bhkv->bhcv', q_n, state)
        state = d_n[..., None] * state + jnp.einsum('bhck,bhcv->bhkv', kt_n, v_n)
        return state, o_n

    s0 = jnp.zeros((B, H, K, V), jnp.float32)
    _, o_inter = lax.scan(step, s0, (q_hat, k_tail, v, decay))
    o = o_intra + o_inter
    return o.transpose(1, 0, 3, 2, 4).reshape(B, S, H, V)


def hgrn2_direction(q, f_logit, v, lb):
    B, S, H, K = q.shape
    f = lb + (1.0 - lb) * jax.nn.sigmoid(f_logit.astype(jnp.float32))
    k = (1.0 - f).reshape(B, S, H, K)
    g = jnp.log(f).reshape(B, S, H, K)
    return gla_chunkwise(q.astype(jnp.float32), k, v.astype(jnp.float32), g)


def rotary(x, positions):
    d = x.shape[-1]
    inv_freq = ROPE_THETA ** (-jnp.arange(0, d, 2, dtype=jnp.float32) / d)
    ang = positions[:, None] * inv_freq[None, :]
    cos = jnp.cos(ang)[None, :, None, :].astype(x.dtype)
    sin = jnp.sin(ang)[None, :, None, :].astype(x.dtype)
    x1, x2 = jnp.split(x, 2, axis=-1)
    return jnp.concatenate([x1 * cos - x2 * sin, x2 * cos + x1 * sin], axis=-1)


def windowed_gqa_sink(q, k, v, sink):
    B, S, H, D = q.shape
    nb = S // BLOCK
    qb = q.reshape(B, nb, BLOCK, KV_HEADS, GQA_GROUP, D)

    def band(t):
        tp = jnp.pad(t, ((0, 0), (BLOCK, BLOCK), (0, 0), (0, 0)))
        tb = tp.reshape(B, nb + 2, BLOCK, KV_HEADS, D)
        return jnp.concatenate([tb[:, :-2], tb[:, 1:-1], tb[:, 2:]], axis=2)

    kw, vw = band(k), band(v)
    s = jnp.einsum('bnqhgd,bnkhd->bnhgqk', qb, kw,
                   preferred_element_type=jnp.float32) * (D ** -0.5)
    r = jnp.arange(BLOCK)[:, None]
    c = jnp.arange(3 * BLOCK)[None, :]
    kpos = jnp.arange(nb)[:, None, None] * BLOCK - BLOCK + c[None]
    mask = (jnp.abs(c - BLOCK - r)[None] <= WINDOW) & (kpos >= 0) & (kpos < S)
    mask = mask[None, :, None, None]
    s = jnp.where(mask, s, -jnp.inf)
    sk = sink.astype(jnp.float32).reshape(KV_HEADS, GQA_GROUP)[None, None, :, :, None, None]
    m = jnp.maximum(jnp.max(s, axis=-1, keepdims=True), sk)
    pr = jnp.exp(s - m)
    denom = jnp.sum(pr, axis=-1, keepdims=True) + jnp.exp(sk - m)
    pr = (pr / denom).astype(v.dtype)
    o = jnp.einsum('bnhgqk,bnkhd->bnqhgd', pr, vw)
    return o.reshape(B, S, H * D)


def setup_inputs(seed: int = 0) -> dict:
    key = jax.random.key(seed)
    ks = jax.random.split(key, 20)
    f32 = jnp.float32
    in_width = sum(in_proj_sizes())

    def nrm(k, shape, scale):
        return jax.random.normal(k, shape, f32) * scale

    def gain(k, shape):
        return 1.0 + 0.05 * jax.random.normal(k, shape, f32)

    return {
        "x": nrm(ks[0], (BATCH, SEQ, D_MODEL), 1.0),
        "p": nrm(ks[1], (DEPTH, BATCH, SEQ, PLE_DIM), 1.0),
        "norm_mix_pre": gain(ks[2], (DEPTH, D_MODEL)),
        "norm_mix_post": gain(ks[3], (DEPTH, D_MODEL)),
        "w_in": nrm(ks[4], (DEPTH, D_MODEL, in_width), D_MODEL ** -0.5),
        "lb_fwd": nrm(ks[5], (DEPTH + 1, HGRN_HEADS * HGRN_EXPAND), 0.1),
        "lb_bwd": nrm(ks[6], (DEPTH + 1, HGRN_HEADS * HGRN_EXPAND), 0.1),
        "hgrn_norm": gain(ks[7], (DEPTH, HGRN_VDIM)),
        "attn_sink": nrm(ks[8], (DEPTH, ATTN_HEADS), 0.5),
        "w_hgrn_proj": nrm(ks[9], (DEPTH, HGRN_WIDTH, D_MODEL), HGRN_WIDTH ** -0.5),
        "w_attn_proj": nrm(ks[10], (DEPTH, ATTN_WIDTH, D_MODEL), ATTN_WIDTH ** -0.5),
        "w_out": nrm(ks[11], (DEPTH, D_MODEL, D_MODEL), D_MODEL ** -0.5),
        "norm_mlp_pre": gain(ks[12], (DEPTH, D_MODEL)),
        "norm_mlp_post": gain(ks[13], (DEPTH, D_MODEL)),
        "w_mlp_up": nrm(ks[14], (DEPTH, D_MODEL, D_FF), D_MODEL ** -0.5),
        "w_mlp_down": nrm(ks[15], (DEPTH, D_FF, D_MODEL), D_FF ** -0.5),
        "w_ple": nrm(ks[16], (DEPTH, PLE_DIM, D_MODEL), PLE_DIM ** -0.5),
        "w_ple_gate": nrm(ks[17], (DEPTH, D_MODEL, D_MODEL), D_MODEL ** -0.5),
        "norm_ple": gain(ks[18], (DEPTH, D_MODEL)),
    }


def reference(x, p, norm_mix_pre, norm_mix_post, w_in, lb_fwd, lb_bwd, hgrn_norm, attn_sink,
              w_hgrn_proj, w_attn_proj, w_out, norm_mlp_pre, norm_mlp_post, w_mlp_up,
              w_mlp_down, w_ple, w_ple_gate, norm_ple):
    B, S, _ = x.shape
    positions = jnp.arange(S, dtype=jnp.float32)
    split_at = np.cumsum(in_proj_sizes())[:-1].tolist()
    for i in range(DEPTH):
        h = rms_norm(x, norm_mix_pre[i])
        proj = jnp.einsum('bsd,de->bse', h, w_in[i])
        hq, hf_f, hf_b, hi, hg, aq, ak, av, ga, gb = jnp.split(proj, split_at, axis=-1)

        q_h = jax.nn.silu(hq).reshape(B, S, HGRN_HEADS, HGRN_EXPAND) * (HGRN_EXPAND ** -0.5)
        v_h = hi.reshape(B, S, HGRN_HEADS, HGRN_VDIM)
        o_f = hgrn2_direction(q_h, hf_f, v_h, layer_lower_bound(lb_fwd, i))
        o_b = jnp.flip(hgrn2_direction(jnp.flip(q_h, 1), jnp.flip(hf_b, 1), jnp.flip(v_h, 1),
                                       layer_lower_bound(lb_bwd, i)), 1)
        o_h = (o_f + o_b).astype(x.dtype)
        o_h = rms_norm(o_h, hgrn_norm[i]) * jax.nn.silu(hg).reshape(B, S, HGRN_HEADS, HGRN_VDIM)
        y_a = jnp.einsum('bse,ed->bsd', o_h.reshape(B, S, HGRN_WIDTH), w_hgrn_proj[i])

        qa = rotary(aq.reshape(B, S, ATTN_HEADS, HEAD_DIM), positions)
        ka = rotary(ak.reshape(B, S, KV_HEADS, HEAD_DIM), positions)
        va = av.reshape(B, S, KV_HEADS, HEAD_DIM)
        o_a = windowed_gqa_sink(qa, ka, va, attn_sink[i])
        y_b = jnp.einsum('bse,ed->bsd', o_a, w_attn_proj[i])

        y = jax.nn.sigmoid(ga) * y_a + jax.nn.sigmoid(gb) * y_b
        mix = jnp.einsum('bsd,de->bse', y, w_out[i])
        x = x + rms_norm(mix, norm_mix_post[i])

        h2 = rms_norm(x, norm_mlp_pre[i])
        u = jnp.square(jax.nn.relu(jnp.einsum('bsd,df->bsf', h2, w_mlp_up[i])))
        d = jnp.einsum('bsf,fd->bsd', u, w_mlp_down[i])
        x = x + rms_norm(d, norm_mlp_post[i])

        e = jnp.einsum('bsp,pd->bsd', p[i].astype(x.dtype), w_ple[i])
        gate = jax.nn.sigmoid(jnp.einsum('bsd,de->bse', x, w_ple_gate[i]))
        x = x + rms_norm(e * gate, norm_ple[i])
    return x
```

```python
import numpy as np
import concourse.bass as bass
import concourse.mybir as mybir
from concourse.bass_utils import run_bass_kernel_spmd

F32 = mybir.dt.float32
BF16 = mybir.dt.bfloat16
AF = mybir.ActivationFunctionType
ALU = mybir.AluOpType
AX = mybir.AxisListType

D = 4096
KT = 32
HW = 2048
NH = 16
DFF = 16384
PLE = 256
EPS = 1e-6
NEG = -30000.0
C_HQ, C_FF, C_FB, C_HI, C_HG, C_AQ, C_AK, C_AV, C_GA, C_GB = (
    0, 2048, 4096, 6144, 8192, 10240, 12288, 12800, 13312, 17408)
NIN = 21504

ENGINES = ("pe", "act", "dve", "pool", "sp")
ENG_ATTR = {"pe": "tensor", "act": "scalar", "dve": "vector", "pool": "gpsimd", "sp": "sync"}
SEM_ROLL = 30000


class Op:
    __slots__ = ("eng", "fn", "deps", "sig", "is_dma", "chan", "pos", "signals")

    def __init__(self, eng, fn, is_dma=False, chan=None):
        self.eng = eng
        self.fn = fn
        self.deps = ()
        self.sig = None
        self.is_dma = is_dma
        self.chan = chan
        self.pos = None
        self.signals = False


class Tens:
    def __init__(self, name, handle, off, size):
        self.name = name
        self.h = handle
        self.off = off
        self.size = size
        self.frontier = {}
        self.inherit = ()

    def __getitem__(self, k):
        return self.h[k]


class Prog:
    def __init__(self, nc):
        self.nc = nc
        self.ops = {e: [] for e in ENGINES}
        self.last_writer = {}
        self.readers = {}
        self.live = []
        self.uid = 0
        self.n_sems = 0
        self.dkeys = {}
        self.bank_rr = 0
        self.dma_rr = {}
        self.chan_last = {}

    def sb(self, name, shape, dtype, off):
        self.uid += 1
        uname = "%s_%d" % (name, self.uid)
        size = int(np.prod(shape[1:])) * mybir.dt.size(dtype)
        assert off + size <= 229376 - 32, (name, off, size)
        h = self.nc.alloc_sbuf_tensor_at(uname, list(shape), dtype, offset=off)
        t = Tens(uname, h, off, size)
        inh = {}
        keep = []
        for o in self.live:
            if o.off < off + size and off < o.off + o.size:
                for op in list(o.frontier.values()) + list(o.inherit):
                    kk = ("chan", op.chan) if op.is_dma else op.eng
                    cur = inh.get(kk)
                    if cur is None or op.pos > cur.pos:
                        inh[kk] = op
                if not (off <= o.off and o.off + o.size <= off + size):
                    keep.append(o)
            else:
                keep.append(o)
        self.live = keep
        t.inherit = tuple(inh.values())
        self.live.append(t)
        return t

    def add(self, eng, fn, reads=(), writes=(), is_dma=False, chan=None):
        op = Op(eng, fn, is_dma, chan)
        op.pos = len(self.ops[eng])
        deps = set()
        lw = self.last_writer
        rdrs = self.readers
        for r in reads:
            w = lw.get(r)
            if w is not None:
                deps.add(w)
        for r in writes:
            w = lw.get(r)
            if w is not None:
                deps.add(w)
            rl = rdrs.get(r)
            if rl:
                deps.update(rl)
        fk = ("chan", chan) if is_dma else eng
        for coll in (reads, writes):
            for r in coll:
                if isinstance(r, tuple) and isinstance(r[0], Tens):
                    t = r[0]
                    if t.inherit:
                        deps.update(t.inherit)
                    t.frontier[fk] = op
        deps.discard(op)
        op.deps = tuple(deps)
        for r in writes:
            lw[r] = op
            rdrs[r] = []
        for r in reads:
            l = rdrs.get(r)
            if l is None:
                rdrs[r] = [op]
            else:
                l.append(op)
        self.ops[eng].append(op)
        return op

    DMA_POOL = {"sp": 44, "pool": 24, "act": 4}

    def dma(self, eng, out, in_, reads=(), writes=(), chan=None):
        i = self.dma_rr.get(eng, 0)
        self.dma_rr[eng] = i + 1
        ch = (eng, i % self.DMA_POOL[eng])
        prev = self.chan_last.get(ch)
        op = self.add(eng, lambda e, out=out, in_=in_: e.dma_start(out=out, in_=in_),
                      reads, writes, is_dma=True, chan=ch)
        if prev is not None and prev not in op.deps:
            op.deps = op.deps + (prev,)
        self.chan_last[ch] = op
        return op

    def dk_w(self, name, idx):
        k = ("D", name, idx)
        self.dkeys.setdefault(name, set()).add(k)
        return k

    def dk_r(self, name):
        return list(self.dkeys.get(name, ()))

    def bank(self):
        b = self.bank_rr
        self.bank_rr = (b + 1) % 8
        return b

    @staticmethod
    def _needs_wait(op, d):
        if d.is_dma:
            return True
        if d.eng != op.eng:
            return True
        if op.eng == "pe":
            return False
        if op.is_dma:
            return True
        return (op.pos - d.pos) <= 3

    def emit(self):
        nc = self.nc
        nw = self._needs_wait
        for e in ENGINES:
            for op in self.ops[e]:
                for d in op.deps:
                    if nw(op, d):
                        d.signals = True
        sem_handles = {}

        def get_sem(key):
            if key not in sem_handles:
                sem_handles[key] = nc.alloc_semaphore("s%d" % len(sem_handles))
            return sem_handles[key]

        chan_count = {}
        for e in ENGINES:
            cnt = 0
            gen = 0
            for op in self.ops[e]:
                if op.is_dma:
                    c = chan_count.get(op.chan, 0) + 16
                    chan_count[op.chan] = c
                    op.sig = (("chan", op.chan), c)
                    get_sem(op.sig[0])
                elif op.signals:
                    cnt += 1
                    if cnt > SEM_ROLL:
                        gen += 1
                        cnt = 1
                    op.sig = (("eng", e, gen), cnt)
                    get_sem(op.sig[0])
        self.n_sems = len(sem_handles)
        final_waits = [(("chan", ch), c) for ch, c in chan_count.items()]
        with nc.Block() as block:
            for e in ENGINES:
                ops = self.ops[e]

                def body(engine, ops=ops, e=e):
                    waited = {}
                    for op in ops:
                        for d in op.deps:
                            if not nw(op, d):
                                continue
                            key, val = d.sig
                            if waited.get(key, 0) >= val:
                                continue
                            engine.wait_ge(sem_handles[key], val)
                            waited[key] = val
                        ins = op.fn(engine)
                        if op.sig is not None:
                            key, val = op.sig
                            ins.then_inc(sem_handles[key], 16 if op.is_dma else 1)
                    if e == "sp":
                        for key, val in final_waits:
                            engine.wait_ge(sem_handles[key], val)

                getattr(block, ENG_ATTR[e])(body)


class Builder:
    def __init__(self, T, dbg=()):
        self.T = T
        self.TH = T + 256
        self.dbg = set(dbg)
        nc = self.nc = bass.Bass("TRN2", target_bir_lowering=False)
        P = self.P = Prog(nc)
        self.base = (nc.SBUF_PARTITION_SIZE_BYTES - nc.sbuf_bytes_remaining + 63) // 64 * 64
        self.off = self.base
        self.ps = [nc.alloc_psum_tensor("ps%d" % i, [128, 512], F32) for i in range(8)]
        self.psb = [p.bitcast(BF16) for p in self.ps]
        self.rr = 0
        self._decl_io()
        self._consts()

    def ein(self, name, shape, dt=F32):
        return self.nc.dram_tensor(name, list(shape), dt, kind="ExternalInput").ap()

    def scr(self, name, shape, dt):
        kind = "ExternalOutput" if name in self.dbg else "Internal"
        return self.nc.dram_tensor(name, list(shape), dt, kind=kind).ap()

    def A(self, name, shape, dt):
        t = self.P.sb(name, shape, dt, self.off)
        self.off = (self.off + t.size + 63) // 64 * 64
        return t

    def reset(self):
        self.off = self.stage_base

    def ring(self, name, n, shape, dt):
        return [self.A("%s%d" % (name, i), shape, dt) for i in range(n)]

    def ev_eng(self):
        self.rr += 1
        return "act" if self.rr % 2 else "dve"

    def copy(self, eng, out, in_, reads, writes):
        if eng == "act":
            return self.P.add("act", lambda e: e.activation(out, in_, AF.Copy), reads, writes)
        return self.P.add(eng, lambda e: e.tensor_copy(out, in_), reads, writes)

    def _decl_io(self):
        T, TH = self.T, self.TH
        e = self.ein
        self.xm = e("xm", [TH, D])
        self.xe = e("xe", [3, T, D])
        self.pin = e("p", [T, PLE])
        self.w_in = e("w_in", [D, NIN])
        self.wfx = e("wfx", [3, D, HW])
        self.lbx = e("lbx", [3, 2, 128, NH])
        self.lbm = e("lbm", [2, 2, 128, NH])
        self.flags = e("flags", [6])
        self.w_hp = e("w_hp", [HW, D])
        self.w_ap = e("w_ap", [HW, D])
        self.w_out = e("w_out", [D, D])
        self.w_up = e("w_up", [D, DFF])
        self.w_dn = e("w_dn", [DFF, D])
        self.w_ple = e("w_ple", [PLE, D])
        self.w_pg = e("w_pg", [D, D])
        self.nv = {k: e(k, [D]) for k in ("n_mix_pre", "n_mix_post", "n_mlp_pre", "n_mlp_post", "n_ple")}
        self.hgn = e("hgn", [128])
        self.sink = e("sink", [NH])
        self.cosT = e("cosT", [128, TH])
        self.sinT = e("sinT", [128, TH])
        self.amask = e("amask", [3, 128, 384])
        self.cst_f = e("cst_f", [128, 128 + 128 + 128])
        self.perm = e("perm", [128, 128])
        self.smask = e("smask", [128, T])
        self.out = self.nc.dram_tensor("out", [T, D], F32, kind="ExternalOutput").ap()
        s = self.scr
        self.HT = s("HT", [D, TH], BF16)
        self.QH = s("QH", [HW, T], BF16)
        self.FF = s("FF", [HW, T], F32)
        self.FB = s("FB", [HW, T], F32)
        self.VH = s("VH", [T, HW], BF16)
        self.GH = s("GH", [T, HW], BF16)
        self.QA = s("QA", [HW, T], BF16)
        self.KA = s("KA", [512, TH], BF16)
        self.VA = s("VA", [TH, 512], BF16)
        self.G = s("G", [2 * D, T], BF16)
        self.HTX = s("HTX", [D, T], BF16)
        self.FX = s("FX", [HW, T], F32)
        self.VX = s("VX", [T, HW], BF16)
        self.SIN = s("SIN", [2, 128, HW], F32)
        self.OHT = s("OHT", [HW, T], BF16)
        self.OAT = s("OAT", [HW, T], BF16)
        self.RAW = s("RAW", [T, D], F32)
        self.X1 = s("X1", [T, D], F32)
        self.H2T = s("H2T", [D, T], BF16)
        self.UT = s("UT", [DFF, T], BF16)
        self.X2 = s("X2", [T, D], F32)
        self.X2T = s("X2T", [D, T], BF16)
        self.PTT = s("PTT", [PLE, T], BF16)

    def _consts(self):
        P = self.P
        self.identf = self.A("identf", [128, 128], F32)
        self.identb = self.A("identb", [128, 128], BF16)
        self.trib = self.A("trib", [128, 2, 128], BF16)
        self.permb = self.A("permb", [128, 128], BF16)
        self.flg = self.A("flg", [128, 6], F32)
        tmp = self.A("ctmp", [128, 384], F32)
        tmp2 = self.A("ctmp2", [128, 128], F32)
        P.dma("sp", tmp[:], self.cst_f, writes=[(tmp, 0)], chan="c0")
        P.dma("sp", tmp2[:], self.perm, writes=[(tmp2, 0)], chan="c1")
        P.dma("sp", self.flg[:], self.flags.partition_broadcast(128), writes=[(self.flg, 0)], chan="c2")
        P.add("dve", lambda e: e.tensor_copy(self.identf[:], tmp[:, 0:128]), [(tmp, 0)], [(self.identf, 0)])
        P.add("dve", lambda e: e.tensor_copy(self.identb[:], tmp[:, 0:128]), [(tmp, 0)], [(self.identb, 0)])
        P.add("dve", lambda e: e.tensor_copy(self.trib[:, 0, :], tmp[:, 128:256]), [(tmp, 0)], [(self.trib, 0)])
        P.add("dve", lambda e: e.tensor_copy(self.trib[:, 1, :], tmp[:, 256:384]), [(tmp, 0)], [(self.trib, 0)])
        P.add("dve", lambda e: e.tensor_copy(self.permb[:], tmp2[:]), [(tmp2, 0)], [(self.permb, 0)])
        self.stage_base = self.off

    def stage_rows(self, tag, ntok, x_src, x_key, raw_src=None, raw_key=None, w_post=None,
                   dst_tm=None, dst_tm_key=None, w_pre=None, dstT=None, dstT_key=None, dstT_col0=0):
        P = self.P
        self.reset()
        ntile = ntok // 128
        xs = self.ring(tag + "xs", 2, [128, D], F32)
        rs = self.ring(tag + "rs", 2, [128, D], F32) if raw_src is not None else None
        junk = self.A(tag + "junk", [128, D], BF16)
        hb = self.ring(tag + "hb", 2, [128, D], BF16) if dstT is not None else None
        wpo = wpr = None
        if w_post is not None:
            wpo = self.A(tag + "wpo", [128, D], F32)
            P.dma("sp", wpo[:], w_post.partition_broadcast(128), writes=[(wpo, 0)], chan=wpo.name)
        if w_pre is not None:
            wpr = self.A(tag + "wpr", [128, D], F32)
            P.dma("sp", wpr[:], w_pre.partition_broadcast(128), writes=[(wpr, 0)], chan=wpr.name)
        st = self.ring(tag + "st", 2, [128, KT, 512], BF16) if dstT is not None else None
        sm = self.ring(tag + "sm", 2, [128, 8], F32)
        x_rk = P.dk_r(x_key) if x_key else []
        raw_rk = P.dk_r(raw_key) if raw_key else []
        def loads(tt):
            x = xs[tt % 2]
            P.dma("sp", x[:], x_src[tt * 128:(tt + 1) * 128, :], reads=x_rk, writes=[(x, 0)], chan=x.name)
            if raw_src is not None:
                r = rs[tt % 2]
                P.dma("sp", r[:], raw_src[tt * 128:(tt + 1) * 128, :], reads=raw_rk, writes=[(r, 0)], chan=r.name)

        loads(0)
        for tt in range(ntile):
            x = xs[tt % 2]
            s = sm[tt % 2]
            if tt + 1 < ntile:
                loads(tt + 1)
            if raw_src is not None:
                r = rs[tt % 2]
                P.add("act", lambda e, r=r, s=s: e.activation(junk[:], r[:], AF.Square, accum_out=s[:, 0:1]),
                      [(r, 0)], [(junk, 0), (s, 0)])
                P.add("dve", lambda e, s=s: e.tensor_scalar(s[:, 1:2], s[:, 0:1], 1.0 / D, EPS, ALU.mult, ALU.add), [(s, 0)], [(s, 1)])
                P.add("act", lambda e, s=s: e.activation(s[:, 2:3], s[:, 1:2], AF.Sqrt), [(s, 1)], [(s, 2)])
                P.add("dve", lambda e, s=s: e.reciprocal(s[:, 3:4], s[:, 2:3]), [(s, 2)], [(s, 3)])
                P.add("dve", lambda e, r=r, s=s: e.scalar_tensor_tensor(r[:], r[:], s[:, 3:4], wpo[:], ALU.mult, ALU.mult),
                      [(r, 0), (s, 3), (wpo, 0)], [(r, 0)])
                P.add("dve", lambda e, r=r, x=x: e.tensor_tensor(x[:], x[:], r[:], ALU.add), [(r, 0), (x, 0)], [(x, 0)])
            if dst_tm is not None:
                P.dma("sp", dst_tm[tt * 128:(tt + 1) * 128, :], x[:], reads=[(x, 0)],
                      writes=[P.dk_w(dst_tm_key, tt)], chan=x.name + "o")
            if dstT is not None:
                h = hb[tt % 2]
                if w_pre is not None:
                    P.add("act", lambda e, x=x, s=s: e.activation(junk[:], x[:], AF.Square, accum_out=s[:, 4:5]),
                          [(x, 0)], [(junk, 0), (s, 4)])
                    P.add("dve", lambda e, s=s: e.tensor_scalar(s[:, 5:6], s[:, 4:5], 1.0 / D, EPS, ALU.mult, ALU.add), [(s, 4)], [(s, 5)])
                    P.add("act", lambda e, s=s: e.activation(s[:, 6:7], s[:, 5:6], AF.Sqrt), [(s, 5)], [(s, 6)])
                    P.add("dve", lambda e, s=s: e.reciprocal(s[:, 7:8], s[:, 6:7]), [(s, 6)], [(s, 7)])
                    P.add("dve", lambda e, x=x, s=s, h=h: e.scalar_tensor_tensor(h[:], x[:], s[:, 7:8], wpr[:], ALU.mult, ALU.mult),
                          [(x, 0), (s, 7), (wpr, 0)], [(h, 0)])
                else:
                    P.add("dve", lambda e, x=x, h=h: e.tensor_copy(h[:], x[:]), [(x, 0)], [(h, 0)])
                g4 = tt % 4
                so = st[(tt // 4) % 2]
                for k4 in range(KT // 4):
                    b = P.bank()
                    for q in range(4):
                        kt = k4 * 4 + q
                        P.add("pe", lambda e, b=b, q=q, kt=kt, h=h: e.transpose(self.psb[b][:, q * 128:(q + 1) * 128], h[:, kt * 128:(kt + 1) * 128], self.identb[:]),
                              [(h, 0), (self.identb, 0)], [("ps", b)])
                    eng = self.ev_eng()
                    outap = so[:, k4 * 4:(k4 + 1) * 4, g4 * 128:(g4 + 1) * 128]
                    inap = self.psb[b][:, 0:512].rearrange("p (q t) -> p q t", q=4)
                    self.copy(eng, outap, inap, [("ps", b)], [(so, (g4, k4))])
                if g4 == 3 or tt == ntile - 1:
                    ng = g4 + 1
                    c0 = dstT_col0 + (tt // 4) * 512
                    dview = dstT.rearrange("(kt p) t -> p kt t", p=128)
                    for kq in range(4):
                        P.dma("sp", dview[:, kq * 8:(kq + 1) * 8, c0:c0 + ng * 128], so[:, kq * 8:(kq + 1) * 8, 0:ng * 128],
                              reads=[(so, (g, k4)) for g in range(ng) for k4 in range(kq * 2, kq * 2 + 2)],
                              writes=[P.dk_w(dstT_key, (tt // 4, kq))], chan=so.name + "o%d" % kq)

    def load_act(self, act, src, key, col0, ncols, kts):
        P = self.P
        sv = src.rearrange("(kt p) t -> p kt t", p=128)
        rk = P.dk_r(key)
        for k0 in range(0, kts, 8):
            k1 = min(kts, k0 + 8)
            P.dma("sp", act[:, k0:k1, 0:ncols], sv[:, k0:k1, col0:col0 + ncols], reads=rk,
                  writes=[(act, kt) for kt in range(k0, k1)], chan="%s_%d" % (act.name, k0))

    def load_w(self, slot, w, k0, kts, c0, wc):
        P = self.P
        wv = w.rearrange("(kt p) n -> p kt n", p=128)
        step = 8
        for a in range(0, kts, step):
            b = min(kts, a + step)
            P.dma("pool", slot[:, a:b, 0:wc], wv[:, k0 + a:k0 + b, c0:c0 + wc],
                  writes=[(slot, kt) for kt in range(a, b)], chan="%s_%d" % (slot.name, a))

    def gemm_B(self, act, kts, ntok, w, cols, wring, wc, epi):
        P = self.P
        ntg = (ntok + 511) // 512
        for ci, c0 in enumerate(cols):
            slot = wring[ci % len(wring)]
            self.load_w(slot, w, 0, kts, c0, wc)
            for half in range(wc // 128):
                banks = [P.bank() for _ in range(ntg)]
                for kt in range(kts):
                    for tg in range(ntg):
                        n0 = tg * 512
                        n1 = min(ntok, n0 + 512)
                        b = banks[tg]
                        P.add("pe", lambda e, b=b, slot=slot, kt=kt, half=half, n0=n0, n1=n1: e.matmul(
                            self.ps[b][:, 0:n1 - n0], slot[:, kt, half * 128:(half + 1) * 128], act[:, kt, n0:n1],
                            start=(kt == 0), stop=(kt == kts - 1)),
                            [(slot, kt), (act, kt)], [("ps", b)])
                for tg in range(ntg):
                    epi(ci, half, tg, banks[tg], min(ntok, tg * 512 + 512) - tg * 512)

    def gemm_A(self, act, kts, ntok, w, cols, wring, wc, epi, ktc=None, pre=None):
        P = self.P
        ktc = ktc or kts
        nkc = kts // ktc
        ntt = ntok // 128
        si = 0
        for ci, c0 in enumerate(cols):
            if nkc == 1:
                slot = wring[si % len(wring)]
                si += 1
                self.load_w(slot, w, 0, kts, c0, wc)
                for tt in range(ntt):
                    b = P.bank()
                    if pre is not None:
                        pre(ci, tt)
                    for kt in range(kts):
                        P.add("pe", lambda e, b=b, slot=slot, kt=kt, tt=tt: e.matmul(
                            self.ps[b][:, 0:wc], act[:, kt, tt * 128:(tt + 1) * 128], slot[:, kt, 0:wc],
                            start=(kt == 0), stop=(kt == kts - 1)),
                            [(slot, kt), (act, kt)], [("ps", b)])
                    epi(ci, tt, b)
            else:
                assert ntt <= 4
                banks = [P.bank() for _ in range(ntt)]
                for kc in range(nkc):
                    slot = wring[si % len(wring)]
                    si += 1
                    self.load_w(slot, w, kc * ktc, ktc, c0, wc)
                    for tt in range(ntt):
                        b = banks[tt]
                        for kt in range(ktc):
                            kg = kc * ktc + kt
                            P.add("pe", lambda e, b=b, slot=slot, kt=kt, kg=kg, tt=tt: e.matmul(
                                self.ps[b][:, 0:wc], act[:, kg, tt * 128:(tt + 1) * 128], slot[:, kt, 0:wc],
                                start=(kg == 0), stop=(kg == kts - 1)),
                                [(slot, kt), (act, kg)], [("ps", b)])
                for tt in range(ntt):
                    epi(ci, tt, banks[tt])

    def stage_inproj(self):
        P, T, TH = self.P, self.T, self.TH
        Th = min(T, 1024)
        nth = Th // 128
        self.reset()
        wring = self.ring("wr", 3, [128, KT, 256], BF16)
        ost = self.ring("ost", 2, [128, Th], F32)
        ostb = self.ring("ostb", 2, [128, Th], BF16)
        ast = self.ring("ast", 2, [128, nth, 256], BF16)
        halo = self.A("halo", [128, KT, 256], BF16)
        hT = self.A("hT", [128, KT, Th], BF16)
        sv = self.HT.rearrange("(kt p) t -> p kt t", p=128)
        rk = P.dk_r("HT")
        for k0 in range(0, KT, 8):
            P.dma("sp", halo[:, k0:k0 + 8, 0:128], sv[:, k0:k0 + 8, 0:128], reads=rk,
                  writes=[(halo, kt) for kt in range(k0, k0 + 8)], chan="halo_a%d" % k0)
            P.dma("sp", halo[:, k0:k0 + 8, 128:256], sv[:, k0:k0 + 8, T + 128:T + 256], reads=rk,
                  writes=[(halo, kt) for kt in range(k0, k0 + 8)], chan="halo_b%d" % k0)
        cnt = [0]

        def epi_halo_k(ci, half, tg, b, n):
            o = ostb[cnt[0] % 2]
            cnt[0] += 1
            self.copy(self.ev_eng(), o[:, 0:256], self.ps[b][:, 0:256], [("ps", b)], [(o, 0)])
            r0 = (ci * 2 + half) * 128
            P.dma("sp", self.KA[r0:r0 + 128, 0:128], o[:, 0:128], reads=[(o, 0)], writes=[P.dk_w("KA", ("h0", r0))], chan=o.name + "a")
            P.dma("sp", self.KA[r0:r0 + 128, T + 128:T + 256], o[:, 128:256], reads=[(o, 0)], writes=[P.dk_w("KA", ("h1", r0))], chan=o.name + "b")

        self.gemm_B(halo, KT, 256, self.w_in, [C_AK, C_AK + 256], wring, 256, epi_halo_k)

        def epi_halo_v(ci, tt, b):
            o = ast[cnt[0] % 2]
            cnt[0] += 1
            self.copy(self.ev_eng(), o[:, 0, :], self.ps[b][:, 0:256], [("ps", b)], [(o, 0)])
            r0 = 0 if tt == 0 else T + 128
            P.dma("sp", self.VA[r0:r0 + 128, ci * 256:(ci + 1) * 256], o[:, 0, :], reads=[(o, 0)],
                  writes=[P.dk_w("VA", ("h", tt, ci))], chan=o.name + "v")

        self.gemm_A(halo, KT, 256, self.w_in, [C_AV, C_AV + 256], wring, 256, epi_halo_v)

        def colsB(c0, n):
            return [c0 + i * 256 for i in range(n // 256)]

        for hf in range(T // Th):
            t0 = hf * Th
            self.load_act(hT, self.HT, "HT", 128 + t0, Th, KT)
            ntg = Th // 512

            def mk_epi_B(dst, dkey, func, f32, dcol, t0=t0):
                def epi(ci, half, tg, b, n):
                    ring = ost if f32 else ostb
                    blk = ci * 2 + half
                    o = ring[blk % 2]
                    n0 = tg * 512
                    if func is None:
                        self.copy(self.ev_eng(), o[:, n0:n0 + n], self.ps[b][:, 0:n], [("ps", b)], [(o, tg)])
                    else:
                        P.add("act", lambda e: e.activation(o[:, n0:n0 + n], self.ps[b][:, 0:n], func), [("ps", b)], [(o, tg)])
                    if tg == ntg - 1:
                        r0 = blk * 128
                        P.dma("sp", dst[r0:r0 + 128, dcol + t0:dcol + t0 + Th], o[:, 0:Th], reads=[(o, g) for g in range(ntg)],
                              writes=[P.dk_w(dkey, ("m", r0, t0))], chan=o.name + "s")
                return epi

            self.gemm_B(hT, KT, Th, self.w_in, colsB(C_HQ, 2048), wring, 256, mk_epi_B(self.QH, "QH", AF.Silu, False, 0))
            self.gemm_B(hT, KT, Th, self.w_in, colsB(C_FF, 2048), wring, 256, mk_epi_B(self.FF, "FF", None, True, 0))
            self.gemm_B(hT, KT, Th, self.w_in, colsB(C_FB, 2048), wring, 256, mk_epi_B(self.FB, "FB", None, True, 0))
            self.gemm_B(hT, KT, Th, self.w_in, colsB(C_AQ, 2048), wring, 256, mk_epi_B(self.QA, "QA", None, False, 0))
            self.gemm_B(hT, KT, Th, self.w_in, colsB(C_AK, 512), wring, 256, mk_epi_B(self.KA, "KA", None, False, 128))
            self.gemm_B(hT, KT, Th, self.w_in, colsB(C_GA, 8192), wring, 256, mk_epi_B(self.G, "G", AF.Sigmoid, False, 0))

            def mk_epi_A(dst, dkey, func, row0, t0=t0):
                def epi(ci, tt, b):
                    o = ast[ci % 2]
                    if func is None:
                        self.copy(self.ev_eng(), o[:, tt, :], self.ps[b][:, 0:256], [("ps", b)], [(o, tt)])
                    else:
                        P.add("act", lambda e: e.activation(o[:, tt, :], self.ps[b][:, 0:256], func), [("ps", b)], [(o, tt)])
                    if tt == nth - 1:
                        dv = dst[row0 + t0:row0 + t0 + Th, ci * 256:(ci + 1) * 256].rearrange("(tt p) c -> p tt c", p=128)
                        P.dma("sp", dv, o[:], reads=[(o, t) for t in range(nth)], writes=[P.dk_w(dkey, ("m", ci, t0))], chan=o.name + "s")
                return epi

            self.gemm_A(hT, KT, Th, self.w_in, colsB(C_HI, 2048), wring, 256, mk_epi_A(self.VH, "VH", None, 0))
            self.gemm_A(hT, KT, Th, self.w_in, colsB(C_HG, 2048), wring, 256, mk_epi_A(self.GH, "GH", AF.Silu, 0))
            self.gemm_A(hT, KT, Th, self.w_in, colsB(C_AV, 512), wring, 256, mk_epi_A(self.VA, "VA", None, 128))

    def hgrn_lb(self, tag, src):
        P = self.P
        pr = self.A(tag + "pr", [128, 2, NH], F32)
        lb = self.A(tag + "lb", [128, 2, NH], F32)
        P.dma("sp", pr[:], src.rearrange("r p h -> p r h"), writes=[(pr, 0)], chan=pr.name)
        P.add("dve", lambda e: e.tensor_tensor(pr[:, 0, :], pr[:, 0, :], pr[:, 1, :], ALU.subtract), [(pr, 0)], [(pr, 0)])
        P.add("act", lambda e: e.activation(lb[:, 0, :], pr[:, 0, :], AF.Sigmoid), [(pr, 0)], [(lb, 0)])
        P.add("dve", lambda e: e.tensor_scalar(lb[:, 1, :], lb[:, 0, :], -1.0, 1.0, ALU.mult, ALU.add), [(lb, 0)], [(lb, 1)])
        return lb

    def stage_hgrn_ext(self):
        P, T = self.P, self.T
        nt = T // 128
        for e_ in range(3):
            self.stage_rows("xa%d" % e_, T, self.xe[e_], None, w_pre=self.nv["n_mix_pre"], dstT=self.HTX, dstT_key="HTX")
            self.reset()
            Th = min(T, 1024)
            nth = Th // 128
            wring = self.ring("xwr", 3, [128, KT, 256], BF16)
            ost = self.ring("xost", 2, [128, Th], F32)
            ast = self.ring("xast", 2, [128, nth, 256], BF16)
            hT = self.A("xhT", [128, KT, Th], BF16)
            for hf in range(T // Th):
                t0 = hf * Th
                ntg = Th // 512
                self.load_act(hT, self.HTX, "HTX", t0, Th, KT)

                def epiB(ci, half, tg, b, n, t0=t0):
                    blk = ci * 2 + half
                    o = ost[blk % 2]
                    n0 = tg * 512
                    self.copy(self.ev_eng(), o[:, n0:n0 + n], self.ps[b][:, 0:n], [("ps", b)], [(o, tg)])
                    if tg == ntg - 1:
                        r0 = blk * 128
                        P.dma("sp", self.FX[r0:r0 + 128, t0:t0 + Th], o[:, 0:Th], reads=[(o, g) for g in range(ntg)],
                              writes=[P.dk_w("FX", (r0, t0))], chan=o.name + "s")

                self.gemm_B(hT, KT, Th, self.wfx[e_], [i * 256 for i in range(8)], wring, 256, epiB)

                def epiA(ci, tt, b, t0=t0):
                    o = ast[ci % 2]
                    self.copy(self.ev_eng(), o[:, tt, :], self.ps[b][:, 0:256], [("ps", b)], [(o, tt)])
                    if tt == nth - 1:
                        dv = self.VX[t0:t0 + Th, ci * 256:(ci + 1) * 256].rearrange("(tt p) c -> p tt c", p=128)
                        P.dma("sp", dv, o[:], reads=[(o, t) for t in range(nth)], writes=[P.dk_w("VX", (ci, t0))], chan=o.name + "s")

                self.gemm_A(hT, KT, Th, self.w_in, [C_HI + i * 256 for i in range(8)], wring, 256, epiA)
            self.reset()
            lb = self.hgrn_lb("xl%d" % e_, self.lbx[e_])
            z = self.ring("xz", 2, [128, T], F32)
            f_r = self.ring("xf", 2, [128, T], F32)
            g_r = self.ring("xg", 2, [128, T], F32)
            kk_r = self.ring("xk", 2, [128, T], F32)
            bb_r = self.ring("xb", 2, [128, T], F32)
            ktl_r = self.ring("xktl", 2, [128, T], BF16)
            sm_r = self.ring("xsm", 2, [128, 4], F32)
            kttm = self.ring("xkttm", 2, [128, nt, 128], BF16)
            vtm = self.ring("xvtm", 2, [128, nt, 128], BF16)
            fx_rk = P.dk_r("FX")
            vx_rk = P.dk_r("VX")
            for h in range(NH):
                zz = z[h % 2]
                v = vtm[h % 2]
                ktm = kttm[h % 2]
                f, g, kk, bb, ktl, sm = f_r[h % 2], g_r[h % 2], kk_r[h % 2], bb_r[h % 2], ktl_r[h % 2], sm_r[h % 2]
                P.dma("sp", zz[:], self.FX[h * 128:(h + 1) * 128, :], reads=fx_rk, writes=[(zz, 0)], chan=zz.name)
                P.dma("sp", v[:], self.VX[:, h * 128:(h + 1) * 128].rearrange("(tt p) c -> p tt c", p=128), reads=vx_rk,
                      writes=[(v, 0)], chan=v.name)
                P.add("act", lambda e, zz=zz, f=f, g=g, kk=kk, bb=bb, ktl=ktl, sm=sm: e.activation(f[:], zz[:], AF.Sigmoid), [(zz, 0)], [(f, 0)])
                P.add("dve", lambda e, h=h, f=f, g=g, kk=kk, bb=bb, ktl=ktl, sm=sm: e.tensor_scalar(f[:], f[:], lb[:, 1, h:h + 1], lb[:, 0, h:h + 1], ALU.mult, ALU.add),
                      [(f, 0), (lb, 0), (lb, 1)], [(f, 0)])
                P.add("act", lambda e, f=f, g=g, kk=kk, bb=bb, ktl=ktl, sm=sm: e.activation(g[:], f[:], AF.Ln), [(f, 0)], [(g, 0)])
                P.add("pool", lambda e, f=f, g=g, kk=kk, bb=bb, ktl=ktl, sm=sm: e.tensor_scalar(kk[:], f[:], -1.0, 1.0, ALU.mult, ALU.add), [(f, 0)], [(kk, 0)])
                P.add("dve", lambda e, f=f, g=g, kk=kk, bb=bb, ktl=ktl, sm=sm: e.tensor_tensor_scan(bb[:], self.ones_t[:, 0:T], g[:], 0.0, ALU.mult, ALU.add),
                      [(g, 0), (self.ones_t, 0)], [(bb, 0)])
                P.add("dve", lambda e, f=f, g=g, kk=kk, bb=bb, ktl=ktl, sm=sm: e.tensor_scalar(g[:], bb[:], -1.0, bb[:, T - 1:T], ALU.mult, ALU.add), [(bb, 0)], [(g, 0)])
                P.add("act", lambda e, f=f, g=g, kk=kk, bb=bb, ktl=ktl, sm=sm: e.activation(g[:], g[:], AF.Exp), [(g, 0)], [(g, 0)])
                P.add("dve", lambda e, f=f, g=g, kk=kk, bb=bb, ktl=ktl, sm=sm: e.tensor_tensor(ktl[:], kk[:], g[:], ALU.mult), [(kk, 0), (g, 0)], [(ktl, 0)])
                P.add("act", lambda e, f=f, g=g, kk=kk, bb=bb, ktl=ktl, sm=sm: e.activation(sm[:, 0:1], bb[:, T - 1:T], AF.Exp), [(bb, 0)], [(sm, 0)])
                for t4 in range(nt // 4):
                    b = P.bank()
                    for q in range(4):
                        tt = t4 * 4 + q
                        P.add("pe", lambda e, b=b, q=q, tt=tt, ktl=ktl: e.transpose(self.psb[b][:, q * 128:(q + 1) * 128], ktl[:, tt * 128:(tt + 1) * 128], self.identb[:]),
                              [(ktl, 0), (self.identb, 0)], [("ps", b)])
                    self.copy(self.ev_eng(), ktm[:, t4 * 4:(t4 + 1) * 4, :], self.psb[b][:, 0:512].rearrange("p (q t) -> p q t", q=4),
                              [("ps", b)], [(ktm, t4)])
                b = P.bank()
                for tt in range(nt):
                    P.add("pe", lambda e, b=b, tt=tt, ktm=ktm, v=v: e.matmul(self.ps[b][:, 0:128], ktm[:, tt, :], v[:, tt, :], start=(tt == 0), stop=(tt == nt - 1)),
                          [(ktm, tt // 4), (v, 0)], [("ps", b)])
                P.add("dve", lambda e, b=b, h=h, sm=sm: e.scalar_tensor_tensor(self.S_run[:, h, :], self.S_run[:, h, :], sm[:, 0:1], self.ps[b][:, 0:128], ALU.mult, ALU.add),
                      [("ps", b), (sm, 0), (self.S_run, h)], [(self.S_run, h)])
                P.add("dve", lambda e, h=h, e_=e_: e.scalar_tensor_tensor(self.S_fin[:, h, :], self.S_run[:, h, :], self.flg[:, 2 * e_:2 * e_ + 1], self.S_fin[:, h, :], ALU.mult, ALU.add),
                      [(self.S_run, h), (self.S_fin, h), (self.flg, 0)], [(self.S_fin, h)])
                P.add("dve", lambda e, h=h, e_=e_: e.tensor_scalar(self.S_run[:, h, :], self.S_run[:, h, :], self.flg[:, 2 * e_ + 1:2 * e_ + 2], None, ALU.mult),
                      [(self.S_run, h), (self.flg, 0)], [(self.S_run, h)])
        P.dma("sp", self.SIN[0], self.S_fin[:].rearrange("p h v -> p (h v)"), reads=[(self.S_fin, h) for h in range(NH)], writes=[P.dk_w("SIN", 0)], chan="sin0")
        P.dma("sp", self.SIN[1], self.S_run[:].rearrange("p h v -> p (h v)"), reads=[(self.S_run, h) for h in range(NH)], writes=[P.dk_w("SIN", 1)], chan="sin1")

    def stage_hgrn_main(self):
        P, T = self.P, self.T
        nt = T // 128
        self.reset()
        lbs = [self.hgrn_lb("ml%d" % d, self.lbm[d]) for d in range(2)]
        smask = self.A("smask", [128, T], F32)
        P.dma("sp", smask[:], self.smask, writes=[(smask, 0)], chan="smask")
        hgw = self.A("hgw", [128, 128], F32)
        P.dma("sp", hgw[:], self.hgn.partition_broadcast(128), writes=[(hgw, 0)], chan="hgw")
        sin = self.ring("sin", 2, [128, 2, 128], F32)
        qq = self.ring("hq", 2, [128, T], BF16)
        vv = self.ring("hv", 2, [128, nt, 128], BF16)
        gg = self.ring("hgt", 2, [128, nt, 128], BF16)
        zz = self.ring("hz", 2, [128, T], F32)
        f = self.A("hf", [128, T], F32)
        g = self.A("hg", [128, T], F32)
        kk = self.A("hk", [128, T], F32)
        bb = self.A("hb", [128, T], F32)
        t2 = self.A("ht2", [128, T], F32)
        qh = self.ring("hqh", 4, [128, T], BF16)
        qi_ = self.ring("hqi", 4, [128, T], BF16)
        kh = self.ring("hkh", 4, [128, T], BF16)
        kttm = self.ring("hkttm", 4, [128, nt, 128], BF16)
        dd = self.ring("hd", 4, [128, nt], F32)
        ktl = self.ring("hktl", 2, [128, T], BF16)
        oacc = self.A("hoacc", [128, nt, 128], F32)
        oaccb = self.A("hoaccb", [128, nt, 128], F32)
        atm6 = self.ring("hatm", 6, [128, 128], BF16)
        Sf = self.ring("hS", 2, [128, 128], F32)
        Sb4 = self.ring("hSb", 4, [128, 128], BF16)
        ost = self.ring("host", 2, [128, T], BF16)
        osm = self.A("hosm", [128, 3, nt], F32)
        on = self.A("hon", [128, nt, 128], BF16)
        junk = self.A("hjunk", [128, 128], F32)
        qk = P.dk_r("QH")
        vk = P.dk_r("VH")
        gk = P.dk_r("GH")
        fk = [P.dk_r("FF"), P.dk_r("FB")]
        sk = P.dk_r("SIN")
        fsrc = [self.FF, self.FB]
        c = 128 ** -0.5
        b3 = lambda ap: ap.rearrange("p (n t) -> p n t", t=128)

        def hloads(h):
            q, v, gt, si = qq[h % 2], vv[h % 2], gg[h % 2], sin[h % 2]
            P.dma("sp", q[:], self.QH[h * 128:(h + 1) * 128, :], reads=qk, writes=[(q, 0)], chan="q")
            P.dma("sp", v[:], self.VH[:, h * 128:(h + 1) * 128].rearrange("(tt p) c -> p tt c", p=128), reads=vk, writes=[(v, 0)], chan="v")
            P.dma("sp", gt[:], self.GH[:, h * 128:(h + 1) * 128].rearrange("(tt p) c -> p tt c", p=128), reads=gk, writes=[(gt, 0)], chan="g")
            for d in range(2):
                P.dma("sp", si[:, d, :], self.SIN[d][:, h * 128:(h + 1) * 128], reads=sk, writes=[(si, d)], chan="s")

        def zload(h, d):
            z = zz[d]
            P.dma("sp", z[:], fsrc[d][h * 128:(h + 1) * 128, :], reads=fk[d], writes=[(z, 0)], chan="z")

        def prep(h, d):
            sl = (h % 2) * 2 + d
            z, lb, q = zz[d], lbs[d], qq[h % 2]
            qd, qi, kd, ktm, dv, ktd = qh[sl], qi_[sl], kh[sl], kttm[sl], dd[sl], ktl[d]
            blast = b3(bb[:])[:, :, 127:128]
            rmid = b3(f[:])[:, :, 63:64]
            th = []
            A_ = th.append
            A_(lambda: P.add("act", lambda e: e.activation(f[:], z[:], AF.Sigmoid), [(z, 0)], [(f, 0)]))
            if h + 1 < NH:
                A_(lambda: zload(h + 1, d))
            A_(lambda: P.add("dve", lambda e: e.tensor_scalar(f[:], f[:], lb[:, 1, h:h + 1], lb[:, 0, h:h + 1], ALU.mult, ALU.add),
                             [(f, 0), (lb, 0), (lb, 1)], [(f, 0)]))
            A_(lambda: P.add("act", lambda e: e.activation(g[:], f[:], AF.Ln), [(f, 0)], [(g, 0)]))
            A_(lambda: P.add("pool", lambda e: e.tensor_scalar(kk[:], f[:], -1.0, 1.0, ALU.mult, ALU.add), [(f, 0)], [(kk, 0)]))
            A_(lambda: P.add("dve", lambda e: e.tensor_tensor_scan(bb[:], smask[:], g[:], 0.0, ALU.mult, ALU.add), [(g, 0), (smask, 0)], [(bb, 0)]))
            if d == 0:
                A_(lambda: P.add("dve", lambda e: e.tensor_tensor(b3(t2[:]), blast.to_broadcast([128, nt, 128]), b3(bb[:]), ALU.subtract), [(bb, 0)], [(t2, 0)]))
                A_(lambda: P.add("pool", lambda e: e.tensor_copy(f[:], bb[:]), [(bb, 0), (kk, 0)], [(f, 0)]))
            else:
                A_(lambda: P.add("dve", lambda e: e.tensor_tensor(t2[:], bb[:], g[:], ALU.subtract), [(bb, 0), (g, 0)], [(t2, 0)]))
                A_(lambda: P.add("dve", lambda e: e.tensor_tensor(b3(f[:]), blast.to_broadcast([128, nt, 128]), b3(t2[:]), ALU.subtract), [(bb, 0), (t2, 0), (kk, 0)], [(f, 0)]))
            A_(lambda: P.add("dve", lambda e: e.tensor_tensor(b3(g[:]), b3(f[:]), rmid.to_broadcast([128, nt, 128]), ALU.subtract), [(f, 0), (g, 0)], [(g, 0)]))
            A_(lambda: P.add("act", lambda e: e.activation(f[:], f[:], AF.Exp), [(f, 0), (g, 0)], [(f, 0)]))
            A_(lambda: P.add("pool", lambda e: e.tensor_tensor(qi[:], q[:], f[:], ALU.mult), [(q, 0), (f, 0)], [(qi, 0)]))
            A_(lambda: P.add("act", lambda e: e.activation(f[:], g[:], AF.Exp), [(g, 0), (qi, 0)], [(f, 0)]))
            A_(lambda: P.add("dve", lambda e: e.tensor_tensor(qd[:], q[:], f[:], ALU.mult), [(q, 0), (f, 0)], [(qd, 0)]))
            A_(lambda: P.add("act", lambda e: e.activation(f[:], g[:], AF.Exp, scale=-1.0), [(g, 0), (qd, 0)], [(f, 0)]))
            A_(lambda: P.add("pool", lambda e: e.tensor_tensor(kd[:], kk[:], f[:], ALU.mult), [(kk, 0), (f, 0)], [(kd, 0)]))
            A_(lambda: P.add("act", lambda e: e.activation(t2[:], t2[:], AF.Exp), [(t2, 0)], [(t2, 0)]))
            A_(lambda: P.add("dve", lambda e: e.tensor_tensor(ktd[:], kk[:], t2[:], ALU.mult), [(kk, 0), (t2, 0)], [(ktd, 0)]))
            A_(lambda: P.add("act", lambda e: e.activation(dv[:].rearrange("p (n o) -> p n o", o=1), blast, AF.Exp), [(bb, 0)], [(dv, 0)]))

            def tr(t4):
                b = P.bank()
                for q_ in range(4):
                    tt = t4 * 4 + q_
                    P.add("pe", lambda e, b=b, q_=q_, tt=tt: e.transpose(self.psb[b][:, q_ * 128:(q_ + 1) * 128], ktd[:, tt * 128:(tt + 1) * 128], self.identb[:]),
                          [(ktd, 0), (self.identb, 0)], [("ps", b)])
                self.copy(self.ev_eng(), ktm[:, t4 * 4:(t4 + 1) * 4, :], self.psb[b][:, 0:512].rearrange("p (q t) -> p q t", q=4),
                          [("ps", b)], [(ktm, t4)])
            for t4 in range(nt // 4):
                A_(lambda t4=t4: tr(t4))
            return th

        def chain_init(h, d):
            S, si = Sf[d], sin[h % 2]
            P.add("dve", lambda e: e.tensor_copy(S[:], si[:, d, :]), [(si, d)], [(S, 0)])
            sb0 = Sb4[d * 2 + 1]
            P.add("act", lambda e: e.activation(sb0[:], S[:], AF.Copy), [(S, 0)], [(sb0, 0)])

        def stA(h, d, i, n):
            sl = (h % 2) * 2 + d
            qd, kd = qh[sl], kh[sl]
            cs = slice(n * 128, (n + 1) * 128)
            b1 = P.bank()
            P.add("pe", lambda e: e.matmul(self.ps[b1][:, 0:128], kd[:, cs], qd[:, cs], start=True, stop=True),
                  [(kd, 0), (qd, 0)], [("ps", b1)])
            a = atm6[d * 3 + i % 3]
            P.add("dve", lambda e: e.tensor_tensor(a[:], self.ps[b1][:, 0:128], self.trib[:, d, :], ALU.mult),
                  [("ps", b1), (self.trib, 0)], [(a, 0)])

        def stB(h, d, i, n):
            sl = (h % 2) * 2 + d
            ktm, dv = kttm[sl], dd[sl]
            S, v = Sf[d], vv[h % 2]
            sbn = Sb4[d * 2 + i % 2]
            b3_ = P.bank()
            P.add("pe", lambda e: e.matmul(self.ps[b3_][:, 0:128], ktm[:, n, :], v[:, n, :], start=True, stop=True),
                  [(ktm, n // 4), (v, 0)], [("ps", b3_)])
            P.add("dve", lambda e: e.scalar_tensor_tensor(S[:], S[:], dv[:, n:n + 1], self.ps[b3_][:, 0:128], ALU.mult, ALU.add),
                  [("ps", b3_), (S, 0), (dv, 0)], [(S, 0)])
            P.add("act", lambda e: e.activation(sbn[:], S[:], AF.Copy), [(S, 0)], [(sbn, 0)])

        def stC(h, d, i, n):
            sl = (h % 2) * 2 + d
            qi = qi_[sl]
            v = vv[h % 2]
            cs = slice(n * 128, (n + 1) * 128)
            a = atm6[d * 3 + i % 3]
            sbp = Sb4[d * 2 + (i - 1) % 2]
            b2 = P.bank()
            P.add("pe", lambda e: e.matmul(self.ps[b2][:, 0:128], a[:], v[:, n, :], start=True, stop=False),
                  [(a, 0), (v, 0)], [("ps", b2)])
            P.add("pe", lambda e: e.matmul(self.ps[b2][:, 0:128], qi[:, cs], sbp[:], start=False, stop=True),
                  [(qi, 0), (sbp, 0)], [("ps", b2)])
            dst = oacc if d == 0 else oaccb
            if (i + d) % 2 == 0:
                P.add("act", lambda e: e.activation(dst[:, n, :], self.ps[b2][:, 0:128], AF.Copy, scale=c), [("ps", b2)], [(dst, n)])
            else:
                P.add("dve", lambda e: e.tensor_scalar(dst[:, n, :], self.ps[b2][:, 0:128], c, None, ALU.mult), [("ps", b2)], [(dst, n)])

        def finalize(h):
            gt = gg[h % 2]
            P.add("dve", lambda e: e.tensor_tensor(oacc[:], oacc[:], oaccb[:], ALU.add),
                  [(oacc, n) for n in range(nt)] + [(oaccb, n) for n in range(nt)], [(oacc, n) for n in range(nt)])
            for n in range(nt):
                P.add("act", lambda e, n=n: e.activation(junk[:], oacc[:, n, :], AF.Square, accum_out=osm[:, 0, n:n + 1]),
                      [(oacc, n)], [(junk, 0), (osm, 0)])
            P.add("dve", lambda e: e.tensor_scalar(osm[:, 1, :], osm[:, 0, :], 1.0 / 128, EPS, ALU.mult, ALU.add), [(osm, 0)], [(osm, 1)])
            P.add("act", lambda e: e.activation(osm[:, 1, :], osm[:, 1, :], AF.Sqrt), [(osm, 1)], [(osm, 1)])
            P.add("dve", lambda e: e.reciprocal(osm[:, 2, :], osm[:, 1, :]), [(osm, 1)], [(osm, 2)])
            allo = [(oacc, n) for n in range(nt)]
            P.add("dve", lambda e: e.tensor_tensor(oacc[:], oacc[:], osm[:, 2, :].rearrange("p (n o) -> p n o", o=1).to_broadcast([128, nt, 128]), ALU.mult),
                  allo + [(osm, 2)], allo)
            P.add("pool", lambda e: e.tensor_tensor(oacc[:], oacc[:], hgw[:].rearrange("p (o v) -> p o v", o=1).to_broadcast([128, nt, 128]), ALU.mult),
                  allo + [(hgw, 0)], allo)
            P.add("dve", lambda e: e.tensor_tensor(on[:], oacc[:], gt[:], ALU.mult), allo + [(gt, 0)], [(on, 0)])
            o = ost[h % 2]
            for t4 in range(nt // 4):
                b = P.bank()
                for q_ in range(4):
                    tt = t4 * 4 + q_
                    P.add("pe", lambda e, b=b, q_=q_, tt=tt: e.transpose(self.psb[b][:, q_ * 128:(q_ + 1) * 128], on[:, tt, :], self.identb[:]),
                          [(on, 0), (self.identb, 0)], [("ps", b)])
                self.copy(self.ev_eng(), o[:, t4 * 512:(t4 + 1) * 512], self.psb[b][:, 0:512], [("ps", b)], [(o, t4)])
            P.dma("sp", self.OHT[h * 128:(h + 1) * 128, :], o[:], reads=[(o, t4) for t4 in range(nt // 4)], writes=[P.dk_w("OHT", h)], chan="o")

        hloads(0)
        zload(0, 0)
        zload(0, 1)
        for th in prep(0, 0) + prep(0, 1):
            th()
        ai = 0
        for h in range(NH):
            if h + 1 < NH:
                hloads(h + 1)
                pend = prep(h + 1, 0) + prep(h + 1, 1)
            else:
                pend = []
            chain_init(h, 0)
            chain_init(h, 1)
            per = (len(pend) + nt - 1) // nt if pend else 0
            nn = lambda d, i: i if d == 0 else nt - 1 - i
            stA(h, 0, 0, nn(0, 0))
            stA(h, 1, 0, nn(1, 0))
            for i in range(nt):
                if i + 1 < nt:
                    stA(h, 0, i + 1, nn(0, i + 1))
                    stA(h, 1, i + 1, nn(1, i + 1))
                stB(h, 0, i, nn(0, i))
                stB(h, 1, i, nn(1, i))
                stC(h, 0, i, nn(0, i))
                stC(h, 1, i, nn(1, i))
                for _ in range(per):
                    if pend:
                        pend.pop(0)()
            while pend:
                pend.pop(0)()
            finalize(h)

    def stage_attn(self):
        P, T, TH = self.P, self.T, self.TH
        nb = T // 128
        self.reset()
        cosT = self.A("cosT", [128, TH], F32)
        sinT = self.A("sinT", [128, TH], F32)
        am = self.A("am", [128, 3, 384], F32)
        snk = self.A("snk", [128, 2, NH], F32)
        P.dma("sp", cosT[:], self.cosT, writes=[(cosT, 0)], chan="cosT")
        P.dma("sp", sinT[:], self.sinT, writes=[(sinT, 0)], chan="sinT")
        P.dma("sp", am[:], self.amask.rearrange("m p k -> p m k"), writes=[(am, 0)], chan="am")
        P.dma("sp", snk[:, 0, :], self.sink.partition_broadcast(128), writes=[(snk, 0)], chan="snk")
        P.add("dve", lambda e: e.tensor_scalar(snk[:, 1, :], snk[:, 0, :], -1.0, None, ALU.mult), [(snk, 0)], [(snk, 1)])
        kraw = self.A("kraw", [128, TH], BF16)
        krot = self.A("krot", [128, TH], BF16)
        vtm = self.A("avtm", [128, TH // 128, 128], BF16)
        qraw = self.ring("qraw", 2, [128, T], BF16)
        qrot = self.ring("qrot", 4, [128, T], BF16)
        tA = self.ring("rtA", 2, [128, 512], F32)
        tB = self.ring("rtB", 2, [128, 512], F32)
        sc = self.ring("asc", 8, [128, 384], F32)
        pr = self.ring("apr", 2, [128, 4, 384], BF16)
        prn = self.ring("aprn", 2, [128, 4, 384], BF16)
        prT = self.ring("aprT", 2, [128, 3, 512], BF16)
        sm = self.ring("asm", 8, [128, 8], F32)
        ost = self.ring("aost", 2, [128, 4, T], BF16)
        scale = 128 ** -0.5
        ka_k, va_k, qa_k = P.dk_r("KA"), P.dk_r("VA"), P.dk_r("QA")
        rc = [0]

        def rotary(raw, rot, ncols, col0):
            for n0 in range(0, ncols, 512):
                n1 = min(ncols, n0 + 512)
                w = n1 - n0
                b = P.bank()
                xa, xb = tA[rc[0] % 2], tB[rc[0] % 2]
                rc[0] += 1
                P.add("pe", lambda e, b=b, n0=n0, n1=n1, w=w: e.matmul(self.ps[b][:, 0:w], self.permb[:], raw[:, n0:n1], start=True, stop=True),
                      [(raw, 0), (self.permb, 0)], [("ps", b)])
                P.add("dve", lambda e, b=b, n0=n0, n1=n1, w=w, xa=xa: e.tensor_tensor(xa[:, 0:w], self.ps[b][:, 0:w], sinT[:, col0 + n0:col0 + n1], ALU.mult),
                      [("ps", b), (sinT, 0)], [(xa, 0)])
                P.add("pool", lambda e, n0=n0, n1=n1, w=w, xb=xb: e.tensor_tensor(xb[:, 0:w], raw[:, n0:n1], cosT[:, col0 + n0:col0 + n1], ALU.mult),
                      [(raw, 0), (cosT, 0)], [(xb, 0)])
                P.add("dve", lambda e, n0=n0, n1=n1, w=w, xa=xa, xb=xb: e.tensor_tensor(rot[:, n0:n1], xa[:, 0:w], xb[:, 0:w], ALU.add),
                      [(xa, 0), (xb, 0)], [(rot, n0 // 512)])

        for gk in range(4):
            P.dma("sp", kraw[:], self.KA[gk * 128:(gk + 1) * 128, :], reads=ka_k, writes=[(kraw, 0)], chan="kraw")
            P.dma("sp", vtm[:], self.VA[:, gk * 128:(gk + 1) * 128].rearrange("(tt p) c -> p tt c", p=128), reads=va_k, writes=[(vtm, 0)], chan="avtm")
            for hh in range(4):
                h = gk * 4 + hh
                qr = qraw[hh % 2]
                P.dma("sp", qr[:], self.QA[h * 128:(h + 1) * 128, :], reads=qa_k, writes=[(qr, 0)], chan="qr")
                if hh == 0:
                    rotary(kraw, krot, TH, 0)
                rotary(qr, qrot[hh], T, 128)
            krk = [(krot, i) for i in range((TH + 511) // 512)]
            o = ost[gk % 2]

            def ph1(n):
                mi = 0 if n == 0 else (2 if n == nb - 1 else 1)
                for hh in range(4):
                    h = gk * 4 + hh
                    s = sm[(n % 2) * 4 + hh]
                    s_ = sc[(n % 2) * 4 + hh]
                    b = P.bank()
                    P.add("pe", lambda e, b=b, hh=hh: e.matmul(self.ps[b][:, 0:384], qrot[hh][:, n * 128:(n + 1) * 128], krot[:, n * 128:n * 128 + 384], start=True, stop=True),
                          [(qrot[hh], i) for i in range(T // 512)] + krk, [("ps", b)])
                    P.add("dve", lambda e, b=b, s_=s_: e.tensor_tensor(s_[:], self.ps[b][:, 0:384], am[:, mi, :], ALU.add), [("ps", b), (am, 0)], [(s_, 0)])
                    P.add("dve", lambda e, s_=s_, s=s: e.tensor_reduce(s[:, 0:1], s_[:], AX.X, ALU.max), [(s_, 0)], [(s, 0)])
                for hh in range(4):
                    h = gk * 4 + hh
                    s = sm[(n % 2) * 4 + hh]
                    P.add("dve", lambda e, s=s, h=h: e.tensor_scalar(s[:, 1:2], s[:, 0:1], -scale, snk[:, 1, h:h + 1], ALU.mult, ALU.min), [(s, 0), (snk, 1)], [(s, 1)])

            def ph2(n):
                p_ = pr[n % 2]
                for hh in range(4):
                    h = gk * 4 + hh
                    s = sm[(n % 2) * 4 + hh]
                    s_ = sc[(n % 2) * 4 + hh]
                    P.add("act", lambda e, s_=s_, s=s, hh=hh: e.activation(p_[:, hh, :], s_[:], AF.Exp, bias=s[:, 1:2], scale=scale, accum_out=s[:, 2:3]),
                          [(s_, 0), (s, 1)], [(p_, hh), (s, 2)])
                    P.add("act", lambda e, s=s, h=h: e.activation(s[:, 3:4], snk[:, 0, h:h + 1], AF.Exp, bias=s[:, 1:2]), [(s, 1), (snk, 0)], [(s, 3)])

            def ph3(n):
                p_, pn = pr[n % 2], prn[n % 2]
                for hh in range(4):
                    s = sm[(n % 2) * 4 + hh]
                    P.add("dve", lambda e, s=s: e.tensor_tensor(s[:, 4:5], s[:, 2:3], s[:, 3:4], ALU.add), [(s, 2), (s, 3)], [(s, 4)])
                for hh in range(4):
                    s = sm[(n % 2) * 4 + hh]
                    P.add("dve", lambda e, s=s: e.reciprocal(s[:, 5:6], s[:, 4:5]), [(s, 4)], [(s, 5)])
                for hh in range(4):
                    s = sm[(n % 2) * 4 + hh]
                    P.add("pool", lambda e, s=s, hh=hh: e.tensor_scalar(pn[:, hh, :], p_[:, hh, :], s[:, 5:6], 0.0, ALU.mult, ALU.add), [(p_, hh), (s, 5)], [(pn, hh)])

            def ph4(n):
                pn, pt = prn[n % 2], prT[n % 2]
                for kb in range(3):
                    b = P.bank()
                    for hh in range(4):
                        P.add("pe", lambda e, b=b, hh=hh, kb=kb: e.transpose(self.psb[b][:, hh * 128:(hh + 1) * 128], pn[:, hh, kb * 128:(kb + 1) * 128], self.identb[:]),
                              [(pn, hh), (self.identb, 0)], [("ps", b)])
                    self.copy(self.ev_eng(), pt[:, kb, :], self.psb[b][:, 0:512], [("ps", b)], [(pt, kb)])
                b = P.bank()
                for kb in range(3):
                    P.add("pe", lambda e, b=b, kb=kb: e.matmul(self.ps[b][:, 0:512], vtm[:, n + kb, :], pt[:, kb, :], start=(kb == 0), stop=(kb == 2)),
                          [(vtm, 0), (pt, kb)], [("ps", b)])
                self.copy(self.ev_eng(), o[:, :, n * 128:(n + 1) * 128], self.ps[b][:, 0:512].rearrange("p (h q) -> p h q", h=4), [("ps", b)], [(o, n)])

            ph1(0)
            for n in range(nb):
                if n + 1 < nb:
                    ph1(n + 1)
                ph2(n)
                ph3(n)
                ph4(n)
            dv = self.OAT[gk * 512:(gk + 1) * 512, :].rearrange("(h p) t -> p h t", p=128)
            P.dma("sp", dv, o[:], reads=[(o, n) for n in range(nb)], writes=[P.dk_w("OAT", gk)], chan="ao")

    def stage_merge(self):
        P, T = self.P, self.T
        Th = min(T, 1024)
        for hf in range(T // Th):
            t0 = hf * Th
            self.reset()
            yT = self.A("myT", [128, KT, Th], BF16)
            gts = self.ring("mg", 4, [128, Th], BF16)
            ta = self.ring("mta", 2, [128, 512], F32)
            tb = self.ring("mtb", 2, [128, 512], F32)
            rst = self.ring("mrst", 2, [128, 512], F32)
            wr = self.ring("mwr", 4, [128, 16, 256], BF16)
            alias_off = self.off
            ohT = self.A("mohT", [128, 16, Th], BF16)
            oaT = self.A("moaT", [128, 16, Th], BF16)
            self.load_act(ohT, self.OHT, "OHT", t0, Th, 16)
            self.load_act(oaT, self.OAT, "OAT", t0, Th, 16)
            gk = P.dk_r("G")
            ntg = Th // 512
            for ci in range(16):
                sa = wr[(2 * ci) % 4]
                sb_ = wr[(2 * ci + 1) % 4]
                self.load_w(sa, self.w_hp, 0, 16, ci * 256, 256)
                self.load_w(sb_, self.w_ap, 0, 16, ci * 256, 256)
                for half in range(2):
                    cb = ci * 2 + half
                    ga = gts[(2 * cb) % 4]
                    gb = gts[(2 * cb + 1) % 4]
                    P.dma("sp", ga[:], self.G[cb * 128:(cb + 1) * 128, t0:t0 + Th], reads=gk, writes=[(ga, 0)], chan=ga.name)
                    P.dma("sp", gb[:], self.G[D + cb * 128:D + (cb + 1) * 128, t0:t0 + Th], reads=gk, writes=[(gb, 0)], chan=gb.name)
                    for tg in range(ntg):
                        ba, bb_ = P.bank(), P.bank()
                        for (bk, slot, act) in ((ba, sa, ohT), (bb_, sb_, oaT)):
                            for kt in range(16):
                                P.add("pe", lambda e, bk=bk, slot=slot, act=act, kt=kt, half=half, tg=tg: e.matmul(
                                    self.ps[bk][:, 0:512], slot[:, kt, half * 128:(half + 1) * 128], act[:, kt, tg * 512:(tg + 1) * 512],
                                    start=(kt == 0), stop=(kt == 15)), [(slot, kt), (act, kt)], [("ps", bk)])
                        sl = slice(tg * 512, (tg + 1) * 512)
                        xa = ta[(cb * ntg + tg) % 2]
                        xb = tb[(cb * ntg + tg) % 2]
                        P.add("dve", lambda e, ba=ba, sl=sl, ga=ga, xa=xa: e.tensor_tensor(xa[:], self.ps[ba][:, 0:512], ga[:, sl], ALU.mult), [("ps", ba), (ga, 0)], [(xa, 0)])
                        P.add("dve", lambda e, bb_=bb_, sl=sl, gb=gb, xb=xb: e.tensor_tensor(xb[:], self.ps[bb_][:, 0:512], gb[:, sl], ALU.mult), [("ps", bb_), (gb, 0)], [(xb, 0)])
                        P.add("dve", lambda e, sl=sl, xa=xa, xb=xb, cb=cb: e.tensor_tensor(yT[:, cb, sl], xa[:], xb[:], ALU.add), [(xa, 0), (xb, 0)], [(yT, cb)])
            self.off = alias_off
            wo = self.ring("mwo", 2, [128, KT, 512], BF16)

            def epi(ci, tt, b, t0=t0):
                o = rst[(ci * (Th // 128) + tt) % 2]
                self.copy(self.ev_eng(), o[:], self.ps[b][:, 0:512], [("ps", b)], [(o, 0)])
                r0 = t0 + tt * 128
                P.dma("sp", self.RAW[r0:r0 + 128, ci * 512:(ci + 1) * 512], o[:], reads=[(o, 0)], writes=[P.dk_w("RAW", (r0, ci))], chan=o.name + "s")
            self.gemm_A(yT, KT, Th, self.w_out, [i * 512 for i in range(8)], wo, 512, epi)

    def stage_mlp(self):
        P, T = self.P, self.T
        self.reset()
        wr = self.ring("uwr", 3, [128, KT, 256], BF16)
        ust = self.ring("ust", 2, [128, T], BF16)
        h2 = self.A("h2T", [128, KT, T], BF16)
        self.load_act(h2, self.H2T, "H2T", 0, T, KT)

        utmp = self.ring("utmp", 2, [128, 512], F32)

        def epi_up(ci, half, tg, b, n):
            blk = ci * 2 + half
            o = ust[blk % 2]
            n0 = tg * 512
            tm = utmp[tg % 2]
            P.add("act", lambda e: e.activation(tm[:, 0:n], self.ps[b][:, 0:n], AF.Relu), [("ps", b)], [(tm, 0)])
            P.add("dve", lambda e: e.tensor_tensor(o[:, n0:n0 + n], tm[:, 0:n], self.ps[b][:, 0:n], ALU.mult), [("ps", b), (tm, 0)], [(o, tg)])
            if n0 + n >= T:
                r0 = blk * 128
                P.dma("sp", self.UT[r0:r0 + 128, :], o[:, 0:T], reads=[(o, g) for g in range((T + 511) // 512)], writes=[P.dk_w("UT", r0)], chan=o.name + "s")

        self.gemm_B(h2, KT, T, self.w_up, [i * 256 for i in range(DFF // 256)], wr, 256, epi_up)
        for tq in range(T // 512):
            t0 = tq * 512
            self.reset()
            wd = self.ring("dwd", 3, [128, 16, 512], BF16)
            rst = self.ring("drst", 2, [128, 512], F32)
            uT = self.A("uT", [128, 128, 512], BF16)
            self.load_act(uT, self.UT, "UT", t0, 512, 128)

            def epi(ci, tt, b, t0=t0):
                o = rst[(ci * 4 + tt) % 2]
                self.copy(self.ev_eng(), o[:], self.ps[b][:, 0:512], [("ps", b)], [(o, 0)])
                r0 = t0 + tt * 128
                P.dma("sp", self.RAW[r0:r0 + 128, ci * 512:(ci + 1) * 512], o[:], reads=[(o, 0)], writes=[P.dk_w("RAW", (r0, ci))], chan=o.name + "s")
            self.gemm_A(uT, 128, 512, self.w_dn, [i * 512 for i in range(8)], wd, 512, epi, ktc=16)

    def stage_ple(self):
        P, T = self.P, self.T
        self.reset()
        ptm = self.A("ptm", [128, T // 128, PLE], F32)
        ptb = self.A("ptb", [128, T // 128, PLE], BF16)
        pst = self.A("pst", [128, 2, T], BF16)
        P.dma("sp", ptm[:], self.pin.rearrange("(tt p) c -> p tt c", p=128), writes=[(ptm, 0)], chan="ptm")
        P.add("dve", lambda e: e.tensor_copy(ptb[:], ptm[:]), [(ptm, 0)], [(ptb, 0)])
        for kt in range(2):
            for t4 in range(T // 512):
                b = P.bank()
                for q in range(4):
                    tt = t4 * 4 + q
                    P.add("pe", lambda e, b=b, q=q, tt=tt, kt=kt: e.transpose(self.psb[b][:, q * 128:(q + 1) * 128], ptb[:, tt, kt * 128:(kt + 1) * 128], self.identb[:]),
                          [(ptb, 0), (self.identb, 0)], [("ps", b)])
                self.copy(self.ev_eng(), pst[:, kt, t4 * 512:(t4 + 1) * 512], self.psb[b][:, 0:512], [("ps", b)], [(pst, (kt, t4))])
        P.dma("sp", self.PTT.rearrange("(kt p) t -> p kt t", p=128), pst[:], reads=[(pst, (kt, t4)) for kt in range(2) for t4 in range(T // 512)],
              writes=[P.dk_w("PTT", 0)], chan="pst")
        Th = min(T, 1024)
        for hf in range(T // Th):
            t0 = hf * Th
            self.reset()
            wg = self.ring("pwg", 2, [128, KT, 512], BF16)
            wp = self.ring("pwp", 2, [128, 2, 512], BF16)
            x2 = self.A("px2T", [128, KT, Th], BF16)
            pT = self.A("ppT", [128, 2, Th], BF16)
            sg = self.ring("psg", 2, [128, 512], F32)
            rst = self.ring("prst", 2, [128, 512], F32)
            self.load_act(x2, self.X2T, "X2T", t0, Th, KT)
            self.load_act(pT, self.PTT, "PTT", t0, Th, 2)
            state = {}

            def pre(ci, tt):
                if tt == 0:
                    slot = wp[ci % 2]
                    self.load_w(slot, self.w_ple, 0, 2, ci * 512, 512)
                    state["wp"] = slot
                slot = state["wp"]
                b = P.bank()
                for kt in range(2):
                    P.add("pe", lambda e, b=b, kt=kt, tt=tt, slot=slot: e.matmul(self.ps[b][:, 0:512], pT[:, kt, tt * 128:(tt + 1) * 128], slot[:, kt, :], start=(kt == 0), stop=(kt == 1)),
                          [(slot, kt), (pT, kt)], [("ps", b)])
                state["eb"] = b

            def epi(ci, tt, b, t0=t0):
                eb = state["eb"]
                s_ = sg[tt % 2]
                o = rst[tt % 2]
                P.add("act", lambda e: e.activation(s_[:], self.ps[b][:, 0:512], AF.Sigmoid), [("ps", b)], [(s_, 0)])
                P.add("dve", lambda e: e.tensor_tensor(o[:], self.ps[eb][:, 0:512], s_[:], ALU.mult), [("ps", eb), (s_, 0)], [(o, 0)])
                r0 = t0 + tt * 128
                P.dma("sp", self.RAW[r0:r0 + 128, ci * 512:(ci + 1) * 512], o[:], reads=[(o, 0)], writes=[P.dk_w("RAW", (r0, ci))], chan=o.name + "s")
            self.gemm_A(x2, KT, Th, self.w_pg, [i * 512 for i in range(8)], wg, 512, epi, pre=pre)

    def build(self, upto=99):
        P, T, TH = self.P, self.T, self.TH
        top = 229376 - 64
        self.ones_t = P.sb("ones_t", [128, T], F32, top - 4 * T)
        P.add("pool", lambda e: e.memset(self.ones_t[:], 1.0), [], [(self.ones_t, 0)])
        self.S_run = P.sb("S_run", [128, NH, 128], F32, top - 4 * T - 8192)
        self.S_fin = P.sb("S_fin", [128, NH, 128], F32, top - 4 * T - 16384)
        P.add("pool", lambda e: e.memset(self.S_run[:], 0.0), [], [(self.S_run, h) for h in range(NH)])
        P.add("pool", lambda e: e.memset(self.S_fin[:], 0.0), [], [(self.S_fin, h) for h in range(NH)])
        self.stage_base = self.off
        self.stage_rows("A", TH, self.xm, None, w_pre=self.nv["n_mix_pre"], dstT=self.HT, dstT_key="HT")
        if upto >= 1:
            self.stage_inproj()
        if upto >= 2:
            self.stage_hgrn_ext()
        if upto >= 3:
            self.stage_hgrn_main()
        if upto >= 4:
            self.stage_attn()
        if upto >= 5:
            self.stage_merge()
            self.stage_rows("N1", T, self.xm[128:128 + T], None, raw_src=self.RAW, raw_key="RAW", w_post=self.nv["n_mix_post"],
                            dst_tm=self.X1, dst_tm_key="X1", w_pre=self.nv["n_mlp_pre"], dstT=self.H2T, dstT_key="H2T")
        if upto >= 6:
            self.stage_mlp()
            self.stage_rows("N2", T, self.X1, "X1", raw_src=self.RAW, raw_key="RAW", w_post=self.nv["n_mlp_post"],
                            dst_tm=self.X2, dst_tm_key="X2", w_pre=None, dstT=self.X2T, dstT_key="X2T")
        if upto >= 7:
            self.stage_ple()
            self.stage_rows("N3", T, self.X2, "X2", raw_src=self.RAW, raw_key="RAW", w_post=self.nv["n_ple"],
                            dst_tm=self.out, dst_tm_key="out")
        P.emit()
        return self.nc


def _host_consts(T, S, j):
    TH = T + 256
    pos = (np.arange(TH) + j * T - 128).astype(np.float32)
    inv = (np.float32(10000.0) ** (-np.arange(0, 128, 2, dtype=np.float32) / np.float32(128))).astype(np.float32)
    ang = (pos[:, None] * inv[None, :]).astype(np.float32)
    cos = np.cos(ang).astype(np.float32).T
    sin = np.sin(ang).astype(np.float32).T
    cosT = np.concatenate([cos, cos], 0)
    sinT = np.concatenate([-sin, sin], 0)
    r = np.arange(128)[:, None]
    cc = np.arange(384)[None, :]
    band = np.abs(cc - 128 - r) <= 128
    am = np.zeros((3, 128, 384), np.float32)
    for mi in range(3):
        ok = band.copy()
        if mi == 0 and j == 0:
            ok[:, 0:128] = False
        if mi == 2 and j == 3:
            ok[:, 256:384] = False
        am[mi] = np.where(ok, 0.0, NEG)
    ident = np.eye(128, dtype=np.float32)
    s_ = np.arange(128)[:, None]
    c_ = np.arange(128)[None, :]
    tri_f = (s_ <= c_).astype(np.float32)
    tri_b = (s_ >= c_).astype(np.float32)
    cst = np.concatenate([ident, tri_f, tri_b], 1)
    perm = np.zeros((128, 128), np.float32)
    for fp in range(128):
        perm[(fp + 64) % 128, fp] = 1.0
    smask = np.ones((128, T), np.float32)
    smask[:, ::128] = 0.0
    return dict(cosT=np.ascontiguousarray(cosT), sinT=np.ascontiguousarray(sinT), amask=am, cst_f=cst, perm=perm, smask=smask)


def make_in_maps(inputs, T):
    x = np.asarray(inputs["x"])
    B, S, _ = x.shape
    assert S == 4 * T
    w_in = np.ascontiguousarray(inputs["w_in"][0])
    f_f = np.ascontiguousarray(w_in[:, C_FF:C_FF + HW])
    f_b = np.ascontiguousarray(w_in[:, C_FB:C_FB + HW])
    shared = dict(
        w_in=w_in, w_hp=np.ascontiguousarray(inputs["w_hgrn_proj"][0]), w_ap=np.ascontiguousarray(inputs["w_attn_proj"][0]),
        w_out=np.ascontiguousarray(inputs["w_out"][0]), w_up=np.ascontiguousarray(inputs["w_mlp_up"][0]),
        w_dn=np.ascontiguousarray(inputs["w_mlp_down"][0]), w_ple=np.ascontiguousarray(inputs["w_ple"][0]),
        w_pg=np.ascontiguousarray(inputs["w_ple_gate"][0]),
        n_mix_pre=np.ascontiguousarray(inputs["norm_mix_pre"][0]), n_mix_post=np.ascontiguousarray(inputs["norm_mix_post"][0]),
        n_mlp_pre=np.ascontiguousarray(inputs["norm_mlp_pre"][0]), n_mlp_post=np.ascontiguousarray(inputs["norm_mlp_post"][0]),
        n_ple=np.ascontiguousarray(inputs["norm_ple"][0]), hgn=np.ascontiguousarray(inputs["hgrn_norm"][0]),
        sink=np.ascontiguousarray(inputs["attn_sink"][0]),
    )
    lbf = np.asarray(inputs["lb_fwd"]).reshape(2, NH, 128).transpose(0, 2, 1)
    lbb = np.asarray(inputs["lb_bwd"]).reshape(2, NH, 128).transpose(0, 2, 1)
    lbm = np.ascontiguousarray(np.stack([lbf, lbb], 0)).astype(np.float32)
    maps = []
    for c in range(8):
        b, j = c // 4, c % 4
        xm = np.zeros((T + 256, D), np.float32)
        lo, hi = j * T - 128, (j + 1) * T + 128
        slo, shi = max(lo, 0), min(hi, S)
        xm[slo - lo:shi - lo] = x[b, slo:shi]
        xe = np.empty((3, T, D), np.float32)
        wfx = np.empty((3, D, HW), np.float32)
        lbx = np.empty((3, 2, 128, NH), np.float32)
        flags = np.zeros(6, np.float32)
        for e in range(3):
            if e < j:
                xe[e] = x[b, e * T:(e + 1) * T]
                wfx[e] = f_f
                lbx[e] = lbf
            else:
                ch = 3 + j - e
                xe[e] = x[b, ch * T:(ch + 1) * T][::-1]
                wfx[e] = f_b
                lbx[e] = lbb
            flags[2 * e] = 1.0 if e == j - 1 else 0.0
            flags[2 * e + 1] = 0.0 if e == j - 1 else 1.0
        m = dict(shared)
        m.update(xm=xm, xe=xe, p=np.ascontiguousarray(inputs["p"][0, b, j * T:(j + 1) * T]), wfx=wfx, lbx=lbx, lbm=lbm, flags=flags)
        m.update(_host_consts(T, S, j))
        maps.append(m)
    return maps


def kernel(**inputs):
    x = np.asarray(inputs["x"])
    B, S, _ = x.shape
    T = S // 4
    nc = Builder(T).build()
    maps = make_in_maps(inputs, T)
    res = run_bass_kernel_spmd(nc, maps, core_ids=list(range(8)))
    out = np.empty((B, S, D), np.float32)
    for c in range(8):
        b, j = c // 4, c % 4
        out[b, j * T:(j + 1) * T] = res.results[c]["out"]
    return out
```

```python
import numpy as np
import concourse.bass as bass
import concourse.mybir as mybir
from concourse.bass_utils import run_bass_kernel_spmd

F32 = mybir.dt.float32
BF16 = mybir.dt.bfloat16
AF = mybir.ActivationFunctionType
ALU = mybir.AluOpType
AX = mybir.AxisListType

D = 4096
KT = 32
HW = 2048
NH = 16
DFF = 16384
PLE = 256
EPS = 1e-6
NEG = -30000.0
C_HQ, C_FF, C_FB, C_HI, C_HG, C_AQ, C_AK, C_AV, C_GA, C_GB = (
    0, 2048, 4096, 6144, 8192, 10240, 12288, 12800, 13312, 17408)
NIN = 21504

ENGINES = ("pe", "act", "dve", "pool", "sp")
ENG_ATTR = {"pe": "tensor", "act": "scalar", "dve": "vector", "pool": "gpsimd", "sp": "sync"}
SEM_ROLL = 30000


class Op:
    __slots__ = ("eng", "fn", "deps", "sig", "is_dma", "chan", "pos", "signals")

    def __init__(self, eng, fn, is_dma=False, chan=None):
        self.eng = eng
        self.fn = fn
        self.deps = ()
        self.sig = None
        self.is_dma = is_dma
        self.chan = chan
        self.pos = None
        self.signals = False


class Tens:
    def __init__(self, name, handle, off, size):
        self.name = name
        self.h = handle
        self.off = off
        self.size = size
        self.frontier = {}
        self.inherit = ()

    def __getitem__(self, k):
        return self.h[k]


class Prog:
    def __init__(self, nc):
        self.nc = nc
        self.ops = {e: [] for e in ENGINES}
        self.last_writer = {}
        self.readers = {}
        self.live = []
        self.uid = 0
        self.n_sems = 0
        self.dkeys = {}
        self.bank_rr = 0
        self.dma_rr = {}
        self.chan_last = {}

    def sb(self, name, shape, dtype, off):
        self.uid += 1
        uname = "%s_%d" % (name, self.uid)
        size = int(np.prod(shape[1:])) * mybir.dt.size(dtype)
        assert off + size <= 229376 - 32, (name, off, size)
        h = self.nc.alloc_sbuf_tensor_at(uname, list(shape), dtype, offset=off)
        t = Tens(uname, h, off, size)
        inh = {}
        keep = []
        for o in self.live:
            if o.off < off + size and off < o.off + o.size:
                for op in list(o.frontier.values()) + list(o.inherit):
                    kk = ("chan", op.chan) if op.is_dma else op.eng
                    cur = inh.get(kk)
                    if cur is None or op.pos > cur.pos:
                        inh[kk] = op
                if not (off <= o.off and o.off + o.size <= off + size):
                    keep.append(o)
            else:
                keep.append(o)
        self.live = keep
        t.inherit = tuple(inh.values())
        self.live.append(t)
        return t

    def add(self, eng, fn, reads=(), writes=(), is_dma=False, chan=None):
        op = Op(eng, fn, is_dma, chan)
        op.pos = len(self.ops[eng])
        deps = set()
        lw = self.last_writer
        rdrs = self.readers
        for r in reads:
            w = lw.get(r)
            if w is not None:
                deps.add(w)
        for r in writes:
            w = lw.get(r)
            if w is not None:
                deps.add(w)
            rl = rdrs.get(r)
            if rl:
                deps.update(rl)
        fk = ("chan", chan) if is_dma else eng
        for coll in (reads, writes):
            for r in coll:
                if isinstance(r, tuple) and isinstance(r[0], Tens):
                    t = r[0]
                    if t.inherit:
                        deps.update(t.inherit)
                    t.frontier[fk] = op
        deps.discard(op)
        op.deps = tuple(deps)
        for r in writes:
            lw[r] = op
            rdrs[r] = []
        for r in reads:
            l = rdrs.get(r)
            if l is None:
                rdrs[r] = [op]
            else:
                l.append(op)
        self.ops[eng].append(op)
        return op

    DMA_POOL = {"sp": 44, "pool": 24, "act": 4}

    def dma(self, eng, out, in_, reads=(), writes=(), chan=None):
        i = self.dma_rr.get(eng, 0)
        self.dma_rr[eng] = i + 1
        ch = (eng, i % self.DMA_POOL[eng])
        prev = self.chan_last.get(ch)
        op = self.add(eng, lambda e, out=out, in_=in_: e.dma_start(out=out, in_=in_),
                      reads, writes, is_dma=True, chan=ch)
        if prev is not None and prev not in op.deps:
            op.deps = op.deps + (prev,)
        self.chan_last[ch] = op
        return op

    def dk_w(self, name, idx):
        k = ("D", name, idx)
        self.dkeys.setdefault(name, set()).add(k)
        return k

    def dk_r(self, name):
        return list(self.dkeys.get(name, ()))

    def bank(self):
        b = self.bank_rr
        self.bank_rr = (b + 1) % 8
        return b

    @staticmethod
    def _needs_wait(op, d):
        if d.is_dma:
            return True
        if d.eng != op.eng:
            return True
        if op.eng == "pe":
            return False
        if op.is_dma:
            return True
        return (op.pos - d.pos) <= 3

    def emit(self):
        nc = self.nc
        nw = self._needs_wait
        for e in ENGINES:
            for op in self.ops[e]:
                for d in op.deps:
                    if nw(op, d):
                        d.signals = True
        sem_handles = {}

        def get_sem(key):
            if key not in sem_handles:
                sem_handles[key] = nc.alloc_semaphore("s%d" % len(sem_handles))
            return sem_handles[key]

        chan_count = {}
        for e in ENGINES:
            cnt = 0
            gen = 0
            for op in self.ops[e]:
                if op.is_dma:
                    c = chan_count.get(op.chan, 0) + 16
                    chan_count[op.chan] = c
                    op.sig = (("chan", op.chan), c)
                    get_sem(op.sig[0])
                elif op.signals:
                    cnt += 1
                    if cnt > SEM_ROLL:
                        gen += 1
                        cnt = 1
                    op.sig = (("eng", e, gen), cnt)
                    get_sem(op.sig[0])
        self.n_sems = len(sem_handles)
        final_waits = [(("chan", ch), c) for ch, c in chan_count.items()]
        with nc.Block() as block:
            for e in ENGINES:
                ops = self.ops[e]

                def body(engine, ops=ops, e=e):
                    waited = {}
                    for op in ops:
                        for d in op.deps:
                            if not nw(op, d):
                                continue
                            key, val = d.sig
                            if waited.get(key, 0) >= val:
                                continue
                            engine.wait_ge(sem_handles[key], val)
                            waited[key] = val
                        ins = op.fn(engine)
                        if op.sig is not None:
                            key, val = op.sig
                            ins.then_inc(sem_handles[key], 16 if op.is_dma else 1)
                    if e == "sp":
                        for key, val in final_waits:
                            engine.wait_ge(sem_handles[key], val)

                getattr(block, ENG_ATTR[e])(body)


class Builder:
    def __init__(self, T, dbg=()):
        self.T = T
        self.TH = T + 256
        self.dbg = set(dbg)
        nc = self.nc = bass.Bass("TRN2", target_bir_lowering=False)
        P = self.P = Prog(nc)
        self.base = (nc.SBUF_PARTITION_SIZE_BYTES - nc.sbuf_bytes_remaining + 63) // 64 * 64
        self.off = self.base
        self.ps = [nc.alloc_psum_tensor("ps%d" % i, [128, 512], F32) for i in range(8)]
        self.psb = [p.bitcast(BF16) for p in self.ps]
        self.rr = 0
        self._decl_io()
        self._consts()

    def ein(self, name, shape, dt=F32):
        return self.nc.dram_tensor(name, list(shape), dt, kind="ExternalInput").ap()

    def scr(self, name, shape, dt):
        kind = "ExternalOutput" if name in self.dbg else "Internal"
        return self.nc.dram_tensor(name, list(shape), dt, kind=kind).ap()

    def A(self, name, shape, dt):
        t = self.P.sb(name, shape, dt, self.off)
        self.off = (self.off + t.size + 63) // 64 * 64
        return t

    def reset(self):
        self.off = self.stage_base

    def ring(self, name, n, shape, dt):
        return [self.A("%s%d" % (name, i), shape, dt) for i in range(n)]

    def ev_eng(self):
        self.rr += 1
        return "act" if self.rr % 2 else "dve"

    def copy(self, eng, out, in_, reads, writes):
        if eng == "act":
            return self.P.add("act", lambda e: e.activation(out, in_, AF.Copy), reads, writes)
        return self.P.add(eng, lambda e: e.tensor_copy(out, in_), reads, writes)

    def _decl_io(self):
        T, TH = self.T, self.TH
        e = self.ein
        self.xm = e("xm", [TH, D])
        self.xe = e("xe", [3, T, D])
        self.pin = e("p", [T, PLE])
        self.w_in = e("w_in", [D, NIN])
        self.wfx = e("wfx", [3, D, HW])
        self.lbx = e("lbx", [3, 2, 128, NH])
        self.lbm = e("lbm", [2, 2, 128, NH])
        self.flags = e("flags", [6])
        self.w_hp = e("w_hp", [HW, D])
        self.w_ap = e("w_ap", [HW, D])
        self.w_out = e("w_out", [D, D])
        self.w_up = e("w_up", [D, DFF])
        self.w_dn = e("w_dn", [DFF, D])
        self.w_ple = e("w_ple", [PLE, D])
        self.w_pg = e("w_pg", [D, D])
        self.nv = {k: e(k, [D]) for k in ("n_mix_pre", "n_mix_post", "n_mlp_pre", "n_mlp_post", "n_ple")}
        self.hgn = e("hgn", [128])
        self.sink = e("sink", [NH])
        self.cosT = e("cosT", [128, TH])
        self.sinT = e("sinT", [128, TH])
        self.amask = e("amask", [3, 128, 384])
        self.cst_f = e("cst_f", [128, 128 + 128 + 128])
        self.perm = e("perm", [128, 128])
        self.smask = e("smask", [128, T])
        self.out = self.nc.dram_tensor("out", [T, D], F32, kind="ExternalOutput").ap()
        s = self.scr
        self.HT = s("HT", [D, TH], BF16)
        self.QH = s("QH", [HW, T], BF16)
        self.FF = s("FF", [HW, T], F32)
        self.FB = s("FB", [HW, T], F32)
        self.VH = s("VH", [T, HW], BF16)
        self.GH = s("GH", [T, HW], BF16)
        self.QA = s("QA", [HW, T], BF16)
        self.KA = s("KA", [512, TH], BF16)
        self.VA = s("VA", [TH, 512], BF16)
        self.G = s("G", [2 * D, T], BF16)
        self.HTX = s("HTX", [D, T], BF16)
        self.FX = s("FX", [HW, T], F32)
        self.VX = s("VX", [T, HW], BF16)
        self.SIN = s("SIN", [2, 128, HW], F32)
        self.OHT = s("OHT", [HW, T], BF16)
        self.OAT = s("OAT", [HW, T], BF16)
        self.RAW = s("RAW", [T, D], F32)
        self.X1 = s("X1", [T, D], F32)
        self.H2T = s("H2T", [D, T], BF16)
        self.UT = s("UT", [DFF, T], BF16)
        self.X2 = s("X2", [T, D], F32)
        self.X2T = s("X2T", [D, T], BF16)
        self.PTT = s("PTT", [PLE, T], BF16)

    def _consts(self):
        P = self.P
        self.identf = self.A("identf", [128, 128], F32)
        self.identb = self.A("identb", [128, 128], BF16)
        self.trib = self.A("trib", [128, 2, 128], BF16)
        self.permb = self.A("permb", [128, 128], BF16)
        self.flg = self.A("flg", [128, 6], F32)
        tmp = self.A("ctmp", [128, 384], F32)
        tmp2 = self.A("ctmp2", [128, 128], F32)
        P.dma("sp", tmp[:], self.cst_f, writes=[(tmp, 0)], chan="c0")
        P.dma("sp", tmp2[:], self.perm, writes=[(tmp2, 0)], chan="c1")
        P.dma("sp", self.flg[:], self.flags.partition_broadcast(128), writes=[(self.flg, 0)], chan="c2")
        P.add("dve", lambda e: e.tensor_copy(self.identf[:], tmp[:, 0:128]), [(tmp, 0)], [(self.identf, 0)])
        P.add("dve", lambda e: e.tensor_copy(self.identb[:], tmp[:, 0:128]), [(tmp, 0)], [(self.identb, 0)])
        P.add("dve", lambda e: e.tensor_copy(self.trib[:, 0, :], tmp[:, 128:256]), [(tmp, 0)], [(self.trib, 0)])
        P.add("dve", lambda e: e.tensor_copy(self.trib[:, 1, :], tmp[:, 256:384]), [(tmp, 0)], [(self.trib, 0)])
        P.add("dve", lambda e: e.tensor_copy(self.permb[:], tmp2[:]), [(tmp2, 0)], [(self.permb, 0)])
        self.stage_base = self.off

    def stage_rows(self, tag, ntok, x_src, x_key, raw_src=None, raw_key=None, w_post=None,
                   dst_tm=None, dst_tm_key=None, w_pre=None, dstT=None, dstT_key=None, dstT_col0=0):
        P = self.P
        self.reset()
        ntile = ntok // 128
        xs = self.ring(tag + "xs", 2, [128, D], F32)
        rs = self.ring(tag + "rs", 2, [128, D], F32) if raw_src is not None else None
        junk = self.A(tag + "junk", [128, D], BF16)
        hb = self.ring(tag + "hb", 2, [128, D], BF16) if dstT is not None else None
        wpo = wpr = None
        if w_post is not None:
            wpo = self.A(tag + "wpo", [128, D], F32)
            P.dma("sp", wpo[:], w_post.partition_broadcast(128), writes=[(wpo, 0)], chan=wpo.name)
        if w_pre is not None:
            wpr = self.A(tag + "wpr", [128, D], F32)
            P.dma("sp", wpr[:], w_pre.partition_broadcast(128), writes=[(wpr, 0)], chan=wpr.name)
        st = self.ring(tag + "st", 2, [128, KT, 512], BF16) if dstT is not None else None
        sm = self.ring(tag + "sm", 2, [128, 8], F32)
        x_rk = P.dk_r(x_key) if x_key else []
        raw_rk = P.dk_r(raw_key) if raw_key else []
        def loads(tt):
            x = xs[tt % 2]
            P.dma("sp", x[:], x_src[tt * 128:(tt + 1) * 128, :], reads=x_rk, writes=[(x, 0)], chan=x.name)
            if raw_src is not None:
                r = rs[tt % 2]
                P.dma("sp", r[:], raw_src[tt * 128:(tt + 1) * 128, :], reads=raw_rk, writes=[(r, 0)], chan=r.name)

        loads(0)
        for tt in range(ntile):
            x = xs[tt % 2]
            s = sm[tt % 2]
            if tt + 1 < ntile:
                loads(tt + 1)
            if raw_src is not None:
                r = rs[tt % 2]
                P.add("act", lambda e, r=r, s=s: e.activation(junk[:], r[:], AF.Square, accum_out=s[:, 0:1]),
                      [(r, 0)], [(junk, 0), (s, 0)])
                P.add("dve", lambda e, s=s: e.tensor_scalar(s[:, 1:2], s[:, 0:1], 1.0 / D, EPS, ALU.mult, ALU.add), [(s, 0)], [(s, 1)])
                P.add("act", lambda e, s=s: e.activation(s[:, 2:3], s[:, 1:2], AF.Sqrt), [(s, 1)], [(s, 2)])
                P.add("dve", lambda e, s=s: e.reciprocal(s[:, 3:4], s[:, 2:3]), [(s, 2)], [(s, 3)])
                P.add("dve", lambda e, r=r, s=s: e.scalar_tensor_tensor(r[:], r[:], s[:, 3:4], wpo[:], ALU.mult, ALU.mult),
                      [(r, 0), (s, 3), (wpo, 0)], [(r, 0)])
                P.add("dve", lambda e, r=r, x=x: e.tensor_tensor(x[:], x[:], r[:], ALU.add), [(r, 0), (x, 0)], [(x, 0)])
            if dst_tm is not None:
                P.dma("sp", dst_tm[tt * 128:(tt + 1) * 128, :], x[:], reads=[(x, 0)],
                      writes=[P.dk_w(dst_tm_key, tt)], chan=x.name + "o")
            if dstT is not None:
                h = hb[tt % 2]
                if w_pre is not None:
                    P.add("act", lambda e, x=x, s=s: e.activation(junk[:], x[:], AF.Square, accum_out=s[:, 4:5]),
                          [(x, 0)], [(junk, 0), (s, 4)])
                    P.add("dve", lambda e, s=s: e.tensor_scalar(s[:, 5:6], s[:, 4:5], 1.0 / D, EPS, ALU.mult, ALU.add), [(s, 4)], [(s, 5)])
                    P.add("act", lambda e, s=s: e.activation(s[:, 6:7], s[:, 5:6], AF.Sqrt), [(s, 5)], [(s, 6)])
                    P.add("dve", lambda e, s=s: e.reciprocal(s[:, 7:8], s[:, 6:7]), [(s, 6)], [(s, 7)])
                    P.add("dve", lambda e, x=x, s=s, h=h: e.scalar_tensor_tensor(h[:], x[:], s[:, 7:8], wpr[:], ALU.mult, ALU.mult),
                          [(x, 0), (s, 7), (wpr, 0)], [(h, 0)])
                else:
                    P.add("dve", lambda e, x=x, h=h: e.tensor_copy(h[:], x[:]), [(x, 0)], [(h, 0)])
                g4 = tt % 4
                so = st[(tt // 4) % 2]
                for k4 in range(KT // 4):
                    b = P.bank()
                    for q in range(4):
                        kt = k4 * 4 + q
                        P.add("pe", lambda e, b=b, q=q, kt=kt, h=h: e.transpose(self.psb[b][:, q * 128:(q + 1) * 128], h[:, kt * 128:(kt + 1) * 128], self.identb[:]),
                              [(h, 0), (self.identb, 0)], [("ps", b)])
                    eng = self.ev_eng()
                    outap = so[:, k4 * 4:(k4 + 1) * 4, g4 * 128:(g4 + 1) * 128]
                    inap = self.psb[b][:, 0:512].rearrange("p (q t) -> p q t", q=4)
                    self.copy(eng, outap, inap, [("ps", b)], [(so, (g4, k4))])
                if g4 == 3 or tt == ntile - 1:
                    ng = g4 + 1
                    c0 = dstT_col0 + (tt // 4) * 512
                    dview = dstT.rearrange("(kt p) t -> p kt t", p=128)
                    for kq in range(4):
                        P.dma("sp", dview[:, kq * 8:(kq + 1) * 8, c0:c0 + ng * 128], so[:, kq * 8:(kq + 1) * 8, 0:ng * 128],
                              reads=[(so, (g, k4)) for g in range(ng) for k4 in range(kq * 2, kq * 2 + 2)],
                              writes=[P.dk_w(dstT_key, (tt // 4, kq))], chan=so.name + "o%d" % kq)

    def load_act(self, act, src, key, col0, ncols, kts):
        P = self.P
        sv = src.rearrange("(kt p) t -> p kt t", p=128)
        rk = P.dk_r(key)
        for k0 in range(0, kts, 8):
            k1 = min(kts, k0 + 8)
            P.dma("sp", act[:, k0:k1, 0:ncols], sv[:, k0:k1, col0:col0 + ncols], reads=rk,
                  writes=[(act, kt) for kt in range(k0, k1)], chan="%s_%d" % (act.name, k0))

    def load_w(self, slot, w, k0, kts, c0, wc):
        P = self.P
        wv = w.rearrange("(kt p) n -> p kt n", p=128)
        step = 8
        for a in range(0, kts, step):
            b = min(kts, a + step)
            P.dma("pool", slot[:, a:b, 0:wc], wv[:, k0 + a:k0 + b, c0:c0 + wc],
                  writes=[(slot, kt) for kt in range(a, b)], chan="%s_%d" % (slot.name, a))

    def gemm_B(self, act, kts, ntok, w, cols, wring, wc, epi):
        P = self.P
        ntg = (ntok + 511) // 512
        for ci, c0 in enumerate(cols):
            slot = wring[ci % len(wring)]
            self.load_w(slot, w, 0, kts, c0, wc)
            for half in range(wc // 128):
                banks = [P.bank() for _ in range(ntg)]
                for kt in range(kts):
                    for tg in range(ntg):
                        n0 = tg * 512
                        n1 = min(ntok, n0 + 512)
                        b = banks[tg]
                        P.add("pe", lambda e, b=b, slot=slot, kt=kt, half=half, n0=n0, n1=n1: e.matmul(
                            self.ps[b][:, 0:n1 - n0], slot[:, kt, half * 128:(half + 1) * 128], act[:, kt, n0:n1],
                            start=(kt == 0), stop=(kt == kts - 1)),
                            [(slot, kt), (act, kt)], [("ps", b)])
                for tg in range(ntg):
                    epi(ci, half, tg, banks[tg], min(ntok, tg * 512 + 512) - tg * 512)

    def gemm_A(self, act, kts, ntok, w, cols, wring, wc, epi, ktc=None, pre=None):
        P = self.P
        ktc = ktc or kts
        nkc = kts // ktc
        ntt = ntok // 128
        si = 0
        for ci, c0 in enumerate(cols):
            if nkc == 1:
                slot = wring[si % len(wring)]
                si += 1
                self.load_w(slot, w, 0, kts, c0, wc)
                for tt in range(ntt):
                    b = P.bank()
                    if pre is not None:
                        pre(ci, tt)
                    for kt in range(kts):
                        P.add("pe", lambda e, b=b, slot=slot, kt=kt, tt=tt: e.matmul(
                            self.ps[b][:, 0:wc], act[:, kt, tt * 128:(tt + 1) * 128], slot[:, kt, 0:wc],
                            start=(kt == 0), stop=(kt == kts - 1)),
                            [(slot, kt), (act, kt)], [("ps", b)])
                    epi(ci, tt, b)
            else:
                assert ntt <= 4
                banks = [P.bank() for _ in range(ntt)]
                for kc in range(nkc):
                    slot = wring[si % len(wring)]
                    si += 1
                    self.load_w(slot, w, kc * ktc, ktc, c0, wc)
                    for tt in range(ntt):
                        b = banks[tt]
                        for kt in range(ktc):
                            kg = kc * ktc + kt
                            P.add("pe", lambda e, b=b, slot=slot, kt=kt, kg=kg, tt=tt: e.matmul(
                                self.ps[b][:, 0:wc], act[:, kg, tt * 128:(tt + 1) * 128], slot[:, kt, 0:wc],
                                start=(kg == 0), stop=(kg == kts - 1)),
                                [(slot, kt), (act, kg)], [("ps", b)])
                for tt in range(ntt):
                    epi(ci, tt, banks[tt])

    def stage_inproj(self):
        P, T, TH = self.P, self.T, self.TH
        Th = min(T, 1024)
        nth = Th // 128
        self.reset()
        wring = self.ring("wr", 3, [128, KT, 256], BF16)
        ost = self.ring("ost", 2, [128, Th], F32)
        ostb = self.ring("ostb", 2, [128, Th], BF16)
        ast = self.ring("ast", 2, [128, nth, 256], BF16)
        halo = self.A("halo", [128, KT, 256], BF16)
        hT = self.A("hT", [128, KT, Th], BF16)
        sv = self.HT.rearrange("(kt p) t -> p kt t", p=128)
        rk = P.dk_r("HT")
        for k0 in range(0, KT, 8):
            P.dma("sp", halo[:, k0:k0 + 8, 0:128], sv[:, k0:k0 + 8, 0:128], reads=rk,
                  writes=[(halo, kt) for kt in range(k0, k0 + 8)], chan="halo_a%d" % k0)
            P.dma("sp", halo[:, k0:k0 + 8, 128:256], sv[:, k0:k0 + 8, T + 128:T + 256], reads=rk,
                  writes=[(halo, kt) for kt in range(k0, k0 + 8)], chan="halo_b%d" % k0)
        cnt = [0]

        def epi_halo_k(ci, half, tg, b, n):
            o = ostb[cnt[0] % 2]
            cnt[0] += 1
            self.copy(self.ev_eng(), o[:, 0:256], self.ps[b][:, 0:256], [("ps", b)], [(o, 0)])
            r0 = (ci * 2 + half) * 128
            P.dma("sp", self.KA[r0:r0 + 128, 0:128], o[:, 0:128], reads=[(o, 0)], writes=[P.dk_w("KA", ("h0", r0))], chan=o.name + "a")
            P.dma("sp", self.KA[r0:r0 + 128, T + 128:T + 256], o[:, 128:256], reads=[(o, 0)], writes=[P.dk_w("KA", ("h1", r0))], chan=o.name + "b")

        self.gemm_B(halo, KT, 256, self.w_in, [C_AK, C_AK + 256], wring, 256, epi_halo_k)

        def epi_halo_v(ci, tt, b):
            o = ast[cnt[0] % 2]
            cnt[0] += 1
            self.copy(self.ev_eng(), o[:, 0, :], self.ps[b][:, 0:256], [("ps", b)], [(o, 0)])
            r0 = 0 if tt == 0 else T + 128
            P.dma("sp", self.VA[r0:r0 + 128, ci * 256:(ci + 1) * 256], o[:, 0, :], reads=[(o, 0)],
                  writes=[P.dk_w("VA", ("h", tt, ci))], chan=o.name + "v")

        self.gemm_A(halo, KT, 256, self.w_in, [C_AV, C_AV + 256], wring, 256, epi_halo_v)

        def colsB(c0, n):
            return [c0 + i * 256 for i in range(n // 256)]

        for hf in range(T // Th):
            t0 = hf * Th
            self.load_act(hT, self.HT, "HT", 128 + t0, Th, KT)
            ntg = Th // 512

            def mk_epi_B(dst, dkey, func, f32, dcol, t0=t0):
                def epi(ci, half, tg, b, n):
                    ring = ost if f32 else ostb
                    blk = ci * 2 + half
                    o = ring[blk % 2]
                    n0 = tg * 512
                    if func is None:
                        self.copy(self.ev_eng(), o[:, n0:n0 + n], self.ps[b][:, 0:n], [("ps", b)], [(o, tg)])
                    else:
                        P.add("act", lambda e: e.activation(o[:, n0:n0 + n], self.ps[b][:, 0:n], func), [("ps", b)], [(o, tg)])
                    if tg == ntg - 1:
                        r0 = blk * 128
                        P.dma("sp", dst[r0:r0 + 128, dcol + t0:dcol + t0 + Th], o[:, 0:Th], reads=[(o, g) for g in range(ntg)],
                              writes=[P.dk_w(dkey, ("m", r0, t0))], chan=o.name + "s")
                return epi

            self.gemm_B(hT, KT, Th, self.w_in, colsB(C_HQ, 2048), wring, 256, mk_epi_B(self.QH, "QH", AF.Silu, False, 0))
            self.gemm_B(hT, KT, Th, self.w_in, colsB(C_FF, 2048), wring, 256, mk_epi_B(self.FF, "FF", None, True, 0))
            self.gemm_B(hT, KT, Th, self.w_in, colsB(C_FB, 2048), wring, 256, mk_epi_B(self.FB, "FB", None, True, 0))
            self.gemm_B(hT, KT, Th, self.w_in, colsB(C_AQ, 2048), wring, 256, mk_epi_B(self.QA, "QA", None, False, 0))
            self.gemm_B(hT, KT, Th, self.w_in, colsB(C_AK, 512), wring, 256, mk_epi_B(self.KA, "KA", None, False, 128))
            self.gemm_B(hT, KT, Th, self.w_in, colsB(C_GA, 8192), wring, 256, mk_epi_B(self.G, "G", AF.Sigmoid, False, 0))

            def mk_epi_A(dst, dkey, func, row0, t0=t0):
                def epi(ci, tt, b):
                    o = ast[ci % 2]
                    if func is None:
                        self.copy(self.ev_eng(), o[:, tt, :], self.ps[b][:, 0:256], [("ps", b)], [(o, tt)])
                    else:
                        P.add("act", lambda e: e.activation(o[:, tt, :], self.ps[b][:, 0:256], func), [("ps", b)], [(o, tt)])
                    if tt == nth - 1:
                        dv = dst[row0 + t0:row0 + t0 + Th, ci * 256:(ci + 1) * 256].rearrange("(tt p) c -> p tt c", p=128)
                        P.dma("sp", dv, o[:], reads=[(o, t) for t in range(nth)], writes=[P.dk_w(dkey, ("m", ci, t0))], chan=o.name + "s")
                return epi

            self.gemm_A(hT, KT, Th, self.w_in, colsB(C_HI, 2048), wring, 256, mk_epi_A(self.VH, "VH", None, 0))
            self.gemm_A(hT, KT, Th, self.w_in, colsB(C_HG, 2048), wring, 256, mk_epi_A(self.GH, "GH", AF.Silu, 0))
            self.gemm_A(hT, KT, Th, self.w_in, colsB(C_AV, 512), wring, 256, mk_epi_A(self.VA, "VA", None, 128))

    def hgrn_lb(self, tag, src):
        P = self.P
        pr = self.A(tag + "pr", [128, 2, NH], F32)
        lb = self.A(tag + "lb", [128, 3, NH], F32)
        P.dma("sp", pr[:], src.rearrange("r p h -> p r h"), writes=[(pr, 0)], chan=pr.name)
        P.add("dve", lambda e: e.tensor_tensor(pr[:, 0, :], pr[:, 0, :], pr[:, 1, :], ALU.subtract), [(pr, 0)], [(pr, 0)])
        P.add("act", lambda e: e.activation(lb[:, 0, :], pr[:, 0, :], AF.Sigmoid), [(pr, 0)], [(lb, 0)])
        P.add("dve", lambda e: e.tensor_scalar(lb[:, 1, :], lb[:, 0, :], -1.0, 1.0, ALU.mult, ALU.add), [(lb, 0)], [(lb, 1)])
        P.add("dve", lambda e: e.tensor_scalar(lb[:, 2, :], lb[:, 0, :], 1.0, -1.0, ALU.mult, ALU.add), [(lb, 0)], [(lb, 2)])
        return lb

    def stage_hgrn_ext(self):
        P, T = self.P, self.T
        nt = T // 128
        for e_ in range(3):
            self.stage_rows("xa%d" % e_, T, self.xe[e_], None, w_pre=self.nv["n_mix_pre"], dstT=self.HTX, dstT_key="HTX")
            self.reset()
            Th = min(T, 1024)
            nth = Th // 128
            wring = self.ring("xwr", 3, [128, KT, 256], BF16)
            ost = self.ring("xost", 2, [128, Th], F32)
            ast = self.ring("xast", 2, [128, nth, 256], BF16)
            hT = self.A("xhT", [128, KT, Th], BF16)
            for hf in range(T // Th):
                t0 = hf * Th
                ntg = Th // 512
                self.load_act(hT, self.HTX, "HTX", t0, Th, KT)

                def epiB(ci, half, tg, b, n, t0=t0):
                    blk = ci * 2 + half
                    o = ost[blk % 2]
                    n0 = tg * 512
                    self.copy(self.ev_eng(), o[:, n0:n0 + n], self.ps[b][:, 0:n], [("ps", b)], [(o, tg)])
                    if tg == ntg - 1:
                        r0 = blk * 128
                        P.dma("sp", self.FX[r0:r0 + 128, t0:t0 + Th], o[:, 0:Th], reads=[(o, g) for g in range(ntg)],
                              writes=[P.dk_w("FX", (r0, t0))], chan=o.name + "s")

                self.gemm_B(hT, KT, Th, self.wfx[e_], [i * 256 for i in range(8)], wring, 256, epiB)

                def epiA(ci, tt, b, t0=t0):
                    o = ast[ci % 2]
                    self.copy(self.ev_eng(), o[:, tt, :], self.ps[b][:, 0:256], [("ps", b)], [(o, tt)])
                    if tt == nth - 1:
                        dv = self.VX[t0:t0 + Th, ci * 256:(ci + 1) * 256].rearrange("(tt p) c -> p tt c", p=128)
                        P.dma("sp", dv, o[:], reads=[(o, t) for t in range(nth)], writes=[P.dk_w("VX", (ci, t0))], chan=o.name + "s")

                self.gemm_A(hT, KT, Th, self.w_in, [C_HI + i * 256 for i in range(8)], wring, 256, epiA)
            self.reset()
            lb = self.hgrn_lb("xl%d" % e_, self.lbx[e_])
            z = self.ring("xz", 2, [128, T], F32)
            f_r = self.ring("xf", 2, [128, T], F32)
            g_r = self.ring("xg", 2, [128, T], F32)
            kk_r = self.ring("xk", 2, [128, T], F32)
            bb_r = self.ring("xb", 2, [128, T], F32)
            ktl_r = self.ring("xktl", 2, [128, T], BF16)
            sm_r = self.ring("xsm", 2, [128, 4], F32)
            kttm = self.ring("xkttm", 2, [128, nt, 128], BF16)
            vtm = self.ring("xvtm", 2, [128, nt, 128], BF16)
            fx_rk = P.dk_r("FX")
            vx_rk = P.dk_r("VX")
            for h in range(NH):
                zz = z[h % 2]
                v = vtm[h % 2]
                ktm = kttm[h % 2]
                f, g, kk, bb, ktl, sm = f_r[h % 2], g_r[h % 2], kk_r[h % 2], bb_r[h % 2], ktl_r[h % 2], sm_r[h % 2]
                P.dma("sp", zz[:], self.FX[h * 128:(h + 1) * 128, :], reads=fx_rk, writes=[(zz, 0)], chan=zz.name)
                P.dma("sp", v[:], self.VX[:, h * 128:(h + 1) * 128].rearrange("(tt p) c -> p tt c", p=128), reads=vx_rk,
                      writes=[(v, 0)], chan=v.name)
                P.add("act", lambda e, zz=zz, f=f, g=g, kk=kk, bb=bb, ktl=ktl, sm=sm: e.activation(f[:], zz[:], AF.Sigmoid), [(zz, 0)], [(f, 0)])
                P.add("act", lambda e, h=h, f=f, g=g, lb=lb: e.activation(g[:], f[:], AF.Ln, bias=lb[:, 0, h:h + 1], scale=lb[:, 1, h:h + 1]),
                      [(f, 0), (lb, 0), (lb, 1)], [(g, 0)])
                P.add("pool", lambda e, h=h, f=f, kk=kk, lb=lb: e.tensor_scalar(kk[:], f[:], lb[:, 2, h:h + 1], lb[:, 1, h:h + 1], ALU.mult, ALU.add),
                      [(f, 0), (lb, 1), (lb, 2)], [(kk, 0)])
                P.add("dve", lambda e, f=f, g=g, kk=kk, bb=bb, ktl=ktl, sm=sm: e.tensor_tensor_scan(bb[:], self.ones_t[:, 0:T], g[:], 0.0, ALU.mult, ALU.add),
                      [(g, 0), (self.ones_t, 0)], [(bb, 0)])
                P.add("dve", lambda e, f=f, g=g, kk=kk, bb=bb, ktl=ktl, sm=sm: e.tensor_scalar(g[:], bb[:], -1.0, bb[:, T - 1:T], ALU.mult, ALU.add), [(bb, 0)], [(g, 0)])
                P.add("act", lambda e, f=f, g=g, kk=kk, bb=bb, ktl=ktl, sm=sm: e.activation(g[:], g[:], AF.Exp), [(g, 0)], [(g, 0)])
                P.add("dve", lambda e, f=f, g=g, kk=kk, bb=bb, ktl=ktl, sm=sm: e.tensor_tensor(ktl[:], kk[:], g[:], ALU.mult), [(kk, 0), (g, 0)], [(ktl, 0)])
                P.add("act", lambda e, f=f, g=g, kk=kk, bb=bb, ktl=ktl, sm=sm: e.activation(sm[:, 0:1], bb[:, T - 1:T], AF.Exp), [(bb, 0)], [(sm, 0)])
                for t4 in range(nt // 4):
                    b = P.bank()
                    for q in range(4):
                        tt = t4 * 4 + q
                        P.add("pe", lambda e, b=b, q=q, tt=tt, ktl=ktl: e.transpose(self.psb[b][:, q * 128:(q + 1) * 128], ktl[:, tt * 128:(tt + 1) * 128], self.identb[:]),
                              [(ktl, 0), (self.identb, 0)], [("ps", b)])
                    self.copy(self.ev_eng(), ktm[:, t4 * 4:(t4 + 1) * 4, :], self.psb[b][:, 0:512].rearrange("p (q t) -> p q t", q=4),
                              [("ps", b)], [(ktm, t4)])
                b = P.bank()
                for tt in range(nt):
                    P.add("pe", lambda e, b=b, tt=tt, ktm=ktm, v=v: e.matmul(self.ps[b][:, 0:128], ktm[:, tt, :], v[:, tt, :], start=(tt == 0), stop=(tt == nt - 1)),
                          [(ktm, tt // 4), (v, 0)], [("ps", b)])
                P.add("dve", lambda e, b=b, h=h, sm=sm: e.scalar_tensor_tensor(self.S_run[:, h, :], self.S_run[:, h, :], sm[:, 0:1], self.ps[b][:, 0:128], ALU.mult, ALU.add),
                      [("ps", b), (sm, 0), (self.S_run, h)], [(self.S_run, h)])
                P.add("dve", lambda e, h=h, e_=e_: e.scalar_tensor_tensor(self.S_fin[:, h, :], self.S_run[:, h, :], self.flg[:, 2 * e_:2 * e_ + 1], self.S_fin[:, h, :], ALU.mult, ALU.add),
                      [(self.S_run, h), (self.S_fin, h), (self.flg, 0)], [(self.S_fin, h)])
                P.add("dve", lambda e, h=h, e_=e_: e.tensor_scalar(self.S_run[:, h, :], self.S_run[:, h, :], self.flg[:, 2 * e_ + 1:2 * e_ + 2], None, ALU.mult),
                      [(self.S_run, h), (self.flg, 0)], [(self.S_run, h)])
        P.dma("sp", self.SIN[0], self.S_fin[:].rearrange("p h v -> p (h v)"), reads=[(self.S_fin, h) for h in range(NH)], writes=[P.dk_w("SIN", 0)], chan="sin0")
        P.dma("sp", self.SIN[1], self.S_run[:].rearrange("p h v -> p (h v)"), reads=[(self.S_run, h) for h in range(NH)], writes=[P.dk_w("SIN", 1)], chan="sin1")

    def stage_hgrn_main(self):
        P, T = self.P, self.T
        nt = T // 128
        self.reset()
        lbs = [self.hgrn_lb("ml%d" % d, self.lbm[d]) for d in range(2)]
        smask = self.A("smask", [128, T], F32)
        P.dma("sp", smask[:], self.smask, writes=[(smask, 0)], chan="smask")
        hgw = self.A("hgw", [128, 128], F32)
        P.dma("sp", hgw[:], self.hgn.partition_broadcast(128), writes=[(hgw, 0)], chan="hgw")
        sin = self.ring("sin", 2, [128, 2, 128], F32)
        qq = self.ring("hq", 2, [128, T], BF16)
        vv = self.ring("hv", 2, [128, nt, 128], BF16)
        gg = self.ring("hgt", 2, [128, nt, 128], BF16)
        zz = self.ring("hz", 2, [128, T], F32)
        f = self.A("hf", [128, T], F32)
        g = self.A("hg", [128, T], F32)
        kk = self.A("hk", [128, T], F32)
        bb = self.A("hb", [128, T], F32)
        t2 = self.A("ht2", [128, T], F32)
        qh = self.ring("hqh", 4, [128, T], BF16)
        qi_ = self.ring("hqi", 4, [128, T], BF16)
        kh = self.ring("hkh", 4, [128, T], BF16)
        kttm = self.ring("hkttm", 4, [128, nt, 128], BF16)
        dd = self.ring("hd", 4, [128, nt], F32)
        ktl = self.ring("hktl", 2, [128, T], BF16)
        oacc = self.A("hoacc", [128, nt, 128], F32)
        oaccb = self.A("hoaccb", [128, nt, 128], F32)
        atm6 = self.ring("hatm", 6, [128, 128], BF16)
        Sf = self.ring("hS", 2, [128, 128], F32)
        Sb4 = self.ring("hSb", 4, [128, 128], BF16)
        ost = self.ring("host", 2, [128, T], BF16)
        osm = self.A("hosm", [128, 3, nt], F32)
        on = self.A("hon", [128, nt, 128], BF16)
        junk = self.A("hjunk", [128, 128], F32)
        qk = P.dk_r("QH")
        vk = P.dk_r("VH")
        gk = P.dk_r("GH")
        fk = [P.dk_r("FF"), P.dk_r("FB")]
        sk = P.dk_r("SIN")
        fsrc = [self.FF, self.FB]
        c = 128 ** -0.5
        b3 = lambda ap: ap.rearrange("p (n t) -> p n t", t=128)

        def hloads(h):
            q, v, gt, si = qq[h % 2], vv[h % 2], gg[h % 2], sin[h % 2]
            P.dma("sp", q[:], self.QH[h * 128:(h + 1) * 128, :], reads=qk, writes=[(q, 0)], chan="q")
            P.dma("sp", v[:], self.VH[:, h * 128:(h + 1) * 128].rearrange("(tt p) c -> p tt c", p=128), reads=vk, writes=[(v, 0)], chan="v")
            P.dma("sp", gt[:], self.GH[:, h * 128:(h + 1) * 128].rearrange("(tt p) c -> p tt c", p=128), reads=gk, writes=[(gt, 0)], chan="g")
            for d in range(2):
                P.dma("sp", si[:, d, :], self.SIN[d][:, h * 128:(h + 1) * 128], reads=sk, writes=[(si, d)], chan="s")

        def zload(h, d):
            z = zz[d]
            P.dma("sp", z[:], fsrc[d][h * 128:(h + 1) * 128, :], reads=fk[d], writes=[(z, 0)], chan="z")

        def prep(h, d):
            sl = (h % 2) * 2 + d
            z, lb, q = zz[d], lbs[d], qq[h % 2]
            qd, qi, kd, ktm, dv, ktd = qh[sl], qi_[sl], kh[sl], kttm[sl], dd[sl], ktl[d]
            blast = b3(bb[:])[:, :, 127:128]
            rt = bb if d == 0 else f
            rmid = b3(rt[:])[:, :, 63:64]
            th = []
            A_ = th.append
            A_(lambda: P.add("act", lambda e: e.activation(f[:], z[:], AF.Sigmoid), [(z, 0)], [(f, 0)]))
            if h + 1 < NH:
                A_(lambda: zload(h + 1, d))
            A_(lambda: P.add("act", lambda e: e.activation(g[:], f[:], AF.Ln, bias=lb[:, 0, h:h + 1], scale=lb[:, 1, h:h + 1]),
                             [(f, 0), (lb, 0), (lb, 1)], [(g, 0)]))
            A_(lambda: P.add("pool", lambda e: e.tensor_scalar(kk[:], f[:], lb[:, 2, h:h + 1], lb[:, 1, h:h + 1], ALU.mult, ALU.add),
                             [(f, 0), (lb, 1), (lb, 2)], [(kk, 0)]))
            A_(lambda: P.add("dve", lambda e: e.tensor_tensor_scan(bb[:], smask[:], g[:], 0.0, ALU.mult, ALU.add), [(g, 0), (smask, 0)], [(bb, 0)]))
            if d == 0:
                A_(lambda: P.add("dve", lambda e: e.tensor_tensor(b3(t2[:]), blast.to_broadcast([128, nt, 128]), b3(bb[:]), ALU.subtract), [(bb, 0)], [(t2, 0)]))
            else:
                A_(lambda: P.add("dve", lambda e: e.tensor_tensor(t2[:], bb[:], g[:], ALU.subtract), [(bb, 0), (g, 0)], [(t2, 0)]))
                A_(lambda: P.add("dve", lambda e: e.tensor_tensor(b3(f[:]), blast.to_broadcast([128, nt, 128]), b3(t2[:]), ALU.subtract), [(bb, 0), (t2, 0), (kk, 0)], [(f, 0)]))
            A_(lambda: P.add("dve", lambda e: e.tensor_tensor(b3(g[:]), b3(rt[:]), rmid.to_broadcast([128, nt, 128]), ALU.subtract), [(rt, 0), (g, 0)], [(g, 0)]))
            A_(lambda: P.add("act", lambda e: e.activation(f[:], rt[:], AF.Exp), [(rt, 0), (g, 0), (kk, 0)], [(f, 0)]))
            A_(lambda: P.add("pool", lambda e: e.tensor_tensor(qi[:], q[:], f[:], ALU.mult), [(q, 0), (f, 0)], [(qi, 0)]))
            A_(lambda: P.add("act", lambda e: e.activation(f[:], g[:], AF.Exp), [(g, 0), (qi, 0)], [(f, 0)]))
            A_(lambda: P.add("dve", lambda e: e.tensor_tensor(qd[:], q[:], f[:], ALU.mult), [(q, 0), (f, 0)], [(qd, 0)]))
            A_(lambda: P.add("act", lambda e: e.activation(f[:], g[:], AF.Exp, scale=-1.0), [(g, 0), (qd, 0)], [(f, 0)]))
            A_(lambda: P.add("pool", lambda e: e.tensor_tensor(kd[:], kk[:], f[:], ALU.mult), [(kk, 0), (f, 0)], [(kd, 0)]))
            A_(lambda: P.add("act", lambda e: e.activation(t2[:], t2[:], AF.Exp), [(t2, 0)], [(t2, 0)]))
            A_(lambda: P.add("dve", lambda e: e.tensor_tensor(ktd[:], kk[:], t2[:], ALU.mult), [(kk, 0), (t2, 0)], [(ktd, 0)]))
            A_(lambda: P.add("act", lambda e: e.activation(dv[:].rearrange("p (n o) -> p n o", o=1), blast, AF.Exp), [(bb, 0)], [(dv, 0)]))

            def tr(t4):
                b = P.bank()
                for q_ in range(4):
                    tt = t4 * 4 + q_
                    P.add("pe", lambda e, b=b, q_=q_, tt=tt: e.transpose(self.psb[b][:, q_ * 128:(q_ + 1) * 128], ktd[:, tt * 128:(tt + 1) * 128], self.identb[:]),
                          [(ktd, 0), (self.identb, 0)], [("ps", b)])
                self.copy(self.ev_eng(), ktm[:, t4 * 4:(t4 + 1) * 4, :], self.psb[b][:, 0:512].rearrange("p (q t) -> p q t", q=4),
                          [("ps", b)], [(ktm, t4)])
            for t4 in range(nt // 4):
                A_(lambda t4=t4: tr(t4))
            return th

        def chain_init(h, d):
            S, si = Sf[d], sin[h % 2]
            P.add("dve", lambda e: e.tensor_copy(S[:], si[:, d, :]), [(si, d)], [(S, 0)])
            sb0 = Sb4[d * 2 + 1]
            P.add("act", lambda e: e.activation(sb0[:], S[:], AF.Copy), [(S, 0)], [(sb0, 0)])

        def stA(h, d, i, n):
            sl = (h % 2) * 2 + d
            qd, kd = qh[sl], kh[sl]
            cs = slice(n * 128, (n + 1) * 128)
            b1 = P.bank()
            P.add("pe", lambda e: e.matmul(self.ps[b1][:, 0:128], kd[:, cs], qd[:, cs], start=True, stop=True),
                  [(kd, 0), (qd, 0)], [("ps", b1)])
            a = atm6[d * 3 + i % 3]
            P.add("dve", lambda e: e.tensor_tensor(a[:], self.ps[b1][:, 0:128], self.trib[:, d, :], ALU.mult),
                  [("ps", b1), (self.trib, 0)], [(a, 0)])

        def stB(h, d, i, n):
            sl = (h % 2) * 2 + d
            ktm, dv = kttm[sl], dd[sl]
            S, v = Sf[d], vv[h % 2]
            sbn = Sb4[d * 2 + i % 2]
            b3_ = P.bank()
            P.add("pe", lambda e: e.matmul(self.ps[b3_][:, 0:128], ktm[:, n, :], v[:, n, :], start=True, stop=True),
                  [(ktm, n // 4), (v, 0)], [("ps", b3_)])
            P.add("dve", lambda e: e.scalar_tensor_tensor(S[:], S[:], dv[:, n:n + 1], self.ps[b3_][:, 0:128], ALU.mult, ALU.add),
                  [("ps", b3_), (S, 0), (dv, 0)], [(S, 0)])
            P.add("act", lambda e: e.activation(sbn[:], S[:], AF.Copy), [(S, 0)], [(sbn, 0)])

        def stC(h, d, i, n):
            sl = (h % 2) * 2 + d
            qi = qi_[sl]
            v = vv[h % 2]
            cs = slice(n * 128, (n + 1) * 128)
            a = atm6[d * 3 + i % 3]
            sbp = Sb4[d * 2 + (i - 1) % 2]
            b2 = P.bank()
            P.add("pe", lambda e: e.matmul(self.ps[b2][:, 0:128], a[:], v[:, n, :], start=True, stop=False),
                  [(a, 0), (v, 0)], [("ps", b2)])
            P.add("pe", lambda e: e.matmul(self.ps[b2][:, 0:128], qi[:, cs], sbp[:], start=False, stop=True),
                  [(qi, 0), (sbp, 0)], [("ps", b2)])
            dst = oacc if d == 0 else oaccb
            if (i + d) % 2 == 0:
                P.add("act", lambda e: e.activation(dst[:, n, :], self.ps[b2][:, 0:128], AF.Copy, scale=c), [("ps", b2)], [(dst, n)])
            else:
                P.add("dve", lambda e: e.tensor_scalar(dst[:, n, :], self.ps[b2][:, 0:128], c, None, ALU.mult), [("ps", b2)], [(dst, n)])

        def finalize(h):
            gt = gg[h % 2]
            P.add("dve", lambda e: e.tensor_tensor(oacc[:], oacc[:], oaccb[:], ALU.add),
                  [(oacc, n) for n in range(nt)] + [(oaccb, n) for n in range(nt)], [(oacc, n) for n in range(nt)])
            for n in range(nt):
                P.add("act", lambda e, n=n: e.activation(junk[:], oacc[:, n, :], AF.Square, accum_out=osm[:, 0, n:n + 1]),
                      [(oacc, n)], [(junk, 0), (osm, 0)])
            P.add("dve", lambda e: e.tensor_scalar(osm[:, 1, :], osm[:, 0, :], 1.0 / 128, EPS, ALU.mult, ALU.add), [(osm, 0)], [(osm, 1)])
            P.add("act", lambda e: e.activation(osm[:, 1, :], osm[:, 1, :], AF.Sqrt), [(osm, 1)], [(osm, 1)])
            P.add("dve", lambda e: e.reciprocal(osm[:, 2, :], osm[:, 1, :]), [(osm, 1)], [(osm, 2)])
            allo = [(oacc, n) for n in range(nt)]
            P.add("dve", lambda e: e.tensor_tensor(oacc[:], oacc[:], osm[:, 2, :].rearrange("p (n o) -> p n o", o=1).to_broadcast([128, nt, 128]), ALU.mult),
                  allo + [(osm, 2)], allo)
            P.add("pool", lambda e: e.tensor_tensor(oacc[:], oacc[:], hgw[:].rearrange("p (o v) -> p o v", o=1).to_broadcast([128, nt, 128]), ALU.mult),
                  allo + [(hgw, 0)], allo)
            P.add("dve", lambda e: e.tensor_tensor(on[:], oacc[:], gt[:], ALU.mult), allo + [(gt, 0)], [(on, 0)])
            o = ost[h % 2]
            for t4 in range(nt // 4):
                b = P.bank()
                for q_ in range(4):
                    tt = t4 * 4 + q_
                    P.add("pe", lambda e, b=b, q_=q_, tt=tt: e.transpose(self.psb[b][:, q_ * 128:(q_ + 1) * 128], on[:, tt, :], self.identb[:]),
                          [(on, 0), (self.identb, 0)], [("ps", b)])
                self.copy(self.ev_eng(), o[:, t4 * 512:(t4 + 1) * 512], self.psb[b][:, 0:512], [("ps", b)], [(o, t4)])
            P.dma("sp", self.OHT[h * 128:(h + 1) * 128, :], o[:], reads=[(o, t4) for t4 in range(nt // 4)], writes=[P.dk_w("OHT", h)], chan="o")

        hloads(0)
        zload(0, 0)
        zload(0, 1)
        for th in prep(0, 0) + prep(0, 1):
            th()
        ai = 0
        for h in range(NH):
            if h + 1 < NH:
                hloads(h + 1)
                pend = prep(h + 1, 0) + prep(h + 1, 1)
            else:
                pend = []
            chain_init(h, 0)
            chain_init(h, 1)
            per = (len(pend) + nt - 1) // nt if pend else 0
            nn = lambda d, i: i if d == 0 else nt - 1 - i
            stA(h, 0, 0, nn(0, 0))
            stA(h, 1, 0, nn(1, 0))
            for i in range(nt):
                if i + 1 < nt:
                    stA(h, 0, i + 1, nn(0, i + 1))
                    stA(h, 1, i + 1, nn(1, i + 1))
                stB(h, 0, i, nn(0, i))
                stB(h, 1, i, nn(1, i))
                stC(h, 0, i, nn(0, i))
                stC(h, 1, i, nn(1, i))
                for _ in range(per):
                    if pend:
                        pend.pop(0)()
            while pend:
                pend.pop(0)()
            finalize(h)

    def stage_attn(self):
        P, T, TH = self.P, self.T, self.TH
        nb = T // 128
        self.reset()
        cosT = self.A("cosT", [128, TH], F32)
        sinT = self.A("sinT", [128, TH], F32)
        am = self.A("am", [128, 3, 384], F32)
        snk = self.A("snk", [128, 2, NH], F32)
        P.dma("sp", cosT[:], self.cosT, writes=[(cosT, 0)], chan="cosT")
        P.dma("sp", sinT[:], self.sinT, writes=[(sinT, 0)], chan="sinT")
        P.dma("sp", am[:], self.amask.rearrange("m p k -> p m k"), writes=[(am, 0)], chan="am")
        P.dma("sp", snk[:, 0, :], self.sink.partition_broadcast(128), writes=[(snk, 0)], chan="snk")
        P.add("dve", lambda e: e.tensor_scalar(snk[:, 1, :], snk[:, 0, :], -1.0, None, ALU.mult), [(snk, 0)], [(snk, 1)])
        kraw = self.A("kraw", [128, TH], BF16)
        krot = self.A("krot", [128, TH], BF16)
        vtm = self.A("avtm", [128, TH // 128, 128], BF16)
        qraw = self.ring("qraw", 2, [128, T], BF16)
        qrot = self.ring("qrot", 4, [128, T], BF16)
        tA = self.ring("rtA", 2, [128, 512], F32)
        tB = self.ring("rtB", 2, [128, 512], F32)
        sc = self.ring("asc", 8, [128, 384], F32)
        pr = self.ring("apr", 2, [128, 4, 384], BF16)
        prn = self.ring("aprn", 2, [128, 4, 384], BF16)
        prT = self.ring("aprT", 2, [128, 3, 512], BF16)
        sm = self.ring("asm", 8, [128, 8], F32)
        ost = self.ring("aost", 2, [128, 4, T], BF16)
        scale = 128 ** -0.5
        ka_k, va_k, qa_k = P.dk_r("KA"), P.dk_r("VA"), P.dk_r("QA")
        rc = [0]

        def rotary(raw, rot, ncols, col0):
            for n0 in range(0, ncols, 512):
                n1 = min(ncols, n0 + 512)
                w = n1 - n0
                b = P.bank()
                xa, xb = tA[rc[0] % 2], tB[rc[0] % 2]
                rc[0] += 1
                P.add("pe", lambda e, b=b, n0=n0, n1=n1, w=w: e.matmul(self.ps[b][:, 0:w], self.permb[:], raw[:, n0:n1], start=True, stop=True),
                      [(raw, 0), (self.permb, 0)], [("ps", b)])
                P.add("dve", lambda e, b=b, n0=n0, n1=n1, w=w, xa=xa: e.tensor_tensor(xa[:, 0:w], self.ps[b][:, 0:w], sinT[:, col0 + n0:col0 + n1], ALU.mult),
                      [("ps", b), (sinT, 0)], [(xa, 0)])
                P.add("pool", lambda e, n0=n0, n1=n1, w=w, xb=xb: e.tensor_tensor(xb[:, 0:w], raw[:, n0:n1], cosT[:, col0 + n0:col0 + n1], ALU.mult),
                      [(raw, 0), (cosT, 0)], [(xb, 0)])
                P.add("dve", lambda e, n0=n0, n1=n1, w=w, xa=xa, xb=xb: e.tensor_tensor(rot[:, n0:n1], xa[:, 0:w], xb[:, 0:w], ALU.add),
                      [(xa, 0), (xb, 0)], [(rot, n0 // 512)])

        for gk in range(4):
            P.dma("sp", kraw[:], self.KA[gk * 128:(gk + 1) * 128, :], reads=ka_k, writes=[(kraw, 0)], chan="kraw")
            P.dma("sp", vtm[:], self.VA[:, gk * 128:(gk + 1) * 128].rearrange("(tt p) c -> p tt c", p=128), reads=va_k, writes=[(vtm, 0)], chan="avtm")
            for hh in range(4):
                h = gk * 4 + hh
                qr = qraw[hh % 2]
                P.dma("sp", qr[:], self.QA[h * 128:(h + 1) * 128, :], reads=qa_k, writes=[(qr, 0)], chan="qr")
                if hh == 0:
                    rotary(kraw, krot, TH, 0)
                rotary(qr, qrot[hh], T, 128)
            krk = [(krot, i) for i in range((TH + 511) // 512)]
            o = ost[gk % 2]

            def ph1(n):
                mi = 0 if n == 0 else (2 if n == nb - 1 else 1)
                for hh in range(4):
                    h = gk * 4 + hh
                    s = sm[(n % 2) * 4 + hh]
                    s_ = sc[(n % 2) * 4 + hh]
                    b = P.bank()
                    P.add("pe", lambda e, b=b, hh=hh: e.matmul(self.ps[b][:, 0:384], qrot[hh][:, n * 128:(n + 1) * 128], krot[:, n * 128:n * 128 + 384], start=True, stop=True),
                          [(qrot[hh], i) for i in range(T // 512)] + krk, [("ps", b)])
                    P.add("dve", lambda e, b=b, s_=s_: e.tensor_tensor(s_[:], self.ps[b][:, 0:384], am[:, mi, :], ALU.add), [("ps", b), (am, 0)], [(s_, 0)])
                    P.add("dve", lambda e, s_=s_, s=s: e.tensor_reduce(s[:, 0:1], s_[:], AX.X, ALU.max), [(s_, 0)], [(s, 0)])
                for hh in range(4):
                    h = gk * 4 + hh
                    s = sm[(n % 2) * 4 + hh]
                    P.add("dve", lambda e, s=s, h=h: e.tensor_scalar(s[:, 1:2], s[:, 0:1], -scale, snk[:, 1, h:h + 1], ALU.mult, ALU.min), [(s, 0), (snk, 1)], [(s, 1)])

            def ph2(n):
                p_ = pr[n % 2]
                for hh in range(4):
                    h = gk * 4 + hh
                    s = sm[(n % 2) * 4 + hh]
                    s_ = sc[(n % 2) * 4 + hh]
                    P.add("act", lambda e, s_=s_, s=s, hh=hh: e.activation(p_[:, hh, :], s_[:], AF.Exp, bias=s[:, 1:2], scale=scale, accum_out=s[:, 2:3]),
                          [(s_, 0), (s, 1)], [(p_, hh), (s, 2)])
                    P.add("act", lambda e, s=s, h=h: e.activation(s[:, 3:4], snk[:, 0, h:h + 1], AF.Exp, bias=s[:, 1:2]), [(s, 1), (snk, 0)], [(s, 3)])

            def ph3(n):
                p_, pn = pr[n % 2], prn[n % 2]
                for hh in range(4):
                    s = sm[(n % 2) * 4 + hh]
                    P.add("dve", lambda e, s=s: e.tensor_tensor(s[:, 4:5], s[:, 2:3], s[:, 3:4], ALU.add), [(s, 2), (s, 3)], [(s, 4)])
                for hh in range(4):
                    s = sm[(n % 2) * 4 + hh]
                    P.add("dve", lambda e, s=s: e.reciprocal(s[:, 5:6], s[:, 4:5]), [(s, 4)], [(s, 5)])
                for hh in range(4):
                    s = sm[(n % 2) * 4 + hh]
                    P.add("pool", lambda e, s=s, hh=hh: e.tensor_scalar(pn[:, hh, :], p_[:, hh, :], s[:, 5:6], 0.0, ALU.mult, ALU.add), [(p_, hh), (s, 5)], [(pn, hh)])

            def ph4(n):
                pn, pt = prn[n % 2], prT[n % 2]
                for kb in range(3):
                    b = P.bank()
                    for hh in range(4):
                        P.add("pe", lambda e, b=b, hh=hh, kb=kb: e.transpose(self.psb[b][:, hh * 128:(hh + 1) * 128], pn[:, hh, kb * 128:(kb + 1) * 128], self.identb[:]),
                              [(pn, hh), (self.identb, 0)], [("ps", b)])
                    self.copy(self.ev_eng(), pt[:, kb, :], self.psb[b][:, 0:512], [("ps", b)], [(pt, kb)])
                b = P.bank()
                for kb in range(3):
                    P.add("pe", lambda e, b=b, kb=kb: e.matmul(self.ps[b][:, 0:512], vtm[:, n + kb, :], pt[:, kb, :], start=(kb == 0), stop=(kb == 2)),
                          [(vtm, 0), (pt, kb)], [("ps", b)])
                self.copy(self.ev_eng(), o[:, :, n * 128:(n + 1) * 128], self.ps[b][:, 0:512].rearrange("p (h q) -> p h q", h=4), [("ps", b)], [(o, n)])

            ph1(0)
            for n in range(nb):
                if n + 1 < nb:
                    ph1(n + 1)
                ph2(n)
                ph3(n)
                ph4(n)
            dv = self.OAT[gk * 512:(gk + 1) * 512, :].rearrange("(h p) t -> p h t", p=128)
            P.dma("sp", dv, o[:], reads=[(o, n) for n in range(nb)], writes=[P.dk_w("OAT", gk)], chan="ao")

    def stage_merge(self):
        P, T = self.P, self.T
        Th = min(T, 1024)
        for hf in range(T // Th):
            t0 = hf * Th
            self.reset()
            yT = self.A("myT", [128, KT, Th], BF16)
            gts = self.ring("mg", 4, [128, Th], BF16)
            ta = self.ring("mta", 2, [128, 512], F32)
            tb = self.ring("mtb", 2, [128, 512], F32)
            rst = self.ring("mrst", 2, [128, 512], F32)
            wr = self.ring("mwr", 4, [128, 16, 256], BF16)
            alias_off = self.off
            ohT = self.A("mohT", [128, 16, Th], BF16)
            oaT = self.A("moaT", [128, 16, Th], BF16)
            self.load_act(ohT, self.OHT, "OHT", t0, Th, 16)
            self.load_act(oaT, self.OAT, "OAT", t0, Th, 16)
            gk = P.dk_r("G")
            ntg = Th // 512
            for ci in range(16):
                sa = wr[(2 * ci) % 4]
                sb_ = wr[(2 * ci + 1) % 4]
                self.load_w(sa, self.w_hp, 0, 16, ci * 256, 256)
                self.load_w(sb_, self.w_ap, 0, 16, ci * 256, 256)
                for half in range(2):
                    cb = ci * 2 + half
                    ga = gts[(2 * cb) % 4]
                    gb = gts[(2 * cb + 1) % 4]
                    P.dma("sp", ga[:], self.G[cb * 128:(cb + 1) * 128, t0:t0 + Th], reads=gk, writes=[(ga, 0)], chan=ga.name)
                    P.dma("sp", gb[:], self.G[D + cb * 128:D + (cb + 1) * 128, t0:t0 + Th], reads=gk, writes=[(gb, 0)], chan=gb.name)
                    for tg in range(ntg):
                        ba, bb_ = P.bank(), P.bank()
                        for (bk, slot, act) in ((ba, sa, ohT), (bb_, sb_, oaT)):
                            for kt in range(16):
                                P.add("pe", lambda e, bk=bk, slot=slot, act=act, kt=kt, half=half, tg=tg: e.matmul(
                                    self.ps[bk][:, 0:512], slot[:, kt, half * 128:(half + 1) * 128], act[:, kt, tg * 512:(tg + 1) * 512],
                                    start=(kt == 0), stop=(kt == 15)), [(slot, kt), (act, kt)], [("ps", bk)])
                        sl = slice(tg * 512, (tg + 1) * 512)
                        xa = ta[(cb * ntg + tg) % 2]
                        xb = tb[(cb * ntg + tg) % 2]
                        P.add("dve", lambda e, ba=ba, sl=sl, ga=ga, xa=xa: e.tensor_tensor(xa[:], self.ps[ba][:, 0:512], ga[:, sl], ALU.mult), [("ps", ba), (ga, 0)], [(xa, 0)])
                        P.add("dve", lambda e, bb_=bb_, sl=sl, gb=gb, xb=xb: e.tensor_tensor(xb[:], self.ps[bb_][:, 0:512], gb[:, sl], ALU.mult), [("ps", bb_), (gb, 0)], [(xb, 0)])
                        P.add("dve", lambda e, sl=sl, xa=xa, xb=xb, cb=cb: e.tensor_tensor(yT[:, cb, sl], xa[:], xb[:], ALU.add), [(xa, 0), (xb, 0)], [(yT, cb)])
            self.off = alias_off
            wo = self.ring("mwo", 2, [128, KT, 512], BF16)

            def epi(ci, tt, b, t0=t0):
                o = rst[(ci * (Th // 128) + tt) % 2]
                self.copy(self.ev_eng(), o[:], self.ps[b][:, 0:512], [("ps", b)], [(o, 0)])
                r0 = t0 + tt * 128
                P.dma("sp", self.RAW[r0:r0 + 128, ci * 512:(ci + 1) * 512], o[:], reads=[(o, 0)], writes=[P.dk_w("RAW", (r0, ci))], chan=o.name + "s")
            self.gemm_A(yT, KT, Th, self.w_out, [i * 512 for i in range(8)], wo, 512, epi)

    def stage_mlp(self):
        P, T = self.P, self.T
        self.reset()
        wr = self.ring("uwr", 3, [128, KT, 256], BF16)
        ust = self.ring("ust", 2, [128, T], BF16)
        h2 = self.A("h2T", [128, KT, T], BF16)
        self.load_act(h2, self.H2T, "H2T", 0, T, KT)

        utmp = self.ring("utmp", 2, [128, 512], F32)

        def epi_up(ci, half, tg, b, n):
            blk = ci * 2 + half
            o = ust[blk % 2]
            n0 = tg * 512
            tm = utmp[tg % 2]
            P.add("act", lambda e: e.activation(tm[:, 0:n], self.ps[b][:, 0:n], AF.Relu), [("ps", b)], [(tm, 0)])
            P.add("dve", lambda e: e.tensor_tensor(o[:, n0:n0 + n], tm[:, 0:n], self.ps[b][:, 0:n], ALU.mult), [("ps", b), (tm, 0)], [(o, tg)])
            if n0 + n >= T:
                r0 = blk * 128
                P.dma("sp", self.UT[r0:r0 + 128, :], o[:, 0:T], reads=[(o, g) for g in range((T + 511) // 512)], writes=[P.dk_w("UT", r0)], chan=o.name + "s")

        self.gemm_B(h2, KT, T, self.w_up, [i * 256 for i in range(DFF // 256)], wr, 256, epi_up)
        for tq in range(T // 512):
            t0 = tq * 512
            self.reset()
            wd = self.ring("dwd", 3, [128, 16, 512], BF16)
            rst = self.ring("drst", 2, [128, 512], F32)
            uT = self.A("uT", [128, 128, 512], BF16)
            self.load_act(uT, self.UT, "UT", t0, 512, 128)

            def epi(ci, tt, b, t0=t0):
                o = rst[(ci * 4 + tt) % 2]
                self.copy(self.ev_eng(), o[:], self.ps[b][:, 0:512], [("ps", b)], [(o, 0)])
                r0 = t0 + tt * 128
                P.dma("sp", self.RAW[r0:r0 + 128, ci * 512:(ci + 1) * 512], o[:], reads=[(o, 0)], writes=[P.dk_w("RAW", (r0, ci))], chan=o.name + "s")
            self.gemm_A(uT, 128, 512, self.w_dn, [i * 512 for i in range(8)], wd, 512, epi, ktc=16)

    def stage_ple(self):
        P, T = self.P, self.T
        self.reset()
        ptm = self.A("ptm", [128, T // 128, PLE], F32)
        ptb = self.A("ptb", [128, T // 128, PLE], BF16)
        pst = self.A("pst", [128, 2, T], BF16)
        P.dma("sp", ptm[:], self.pin.rearrange("(tt p) c -> p tt c", p=128), writes=[(ptm, 0)], chan="ptm")
        P.add("dve", lambda e: e.tensor_copy(ptb[:], ptm[:]), [(ptm, 0)], [(ptb, 0)])
        for kt in range(2):
            for t4 in range(T // 512):
                b = P.bank()
                for q in range(4):
                    tt = t4 * 4 + q
                    P.add("pe", lambda e, b=b, q=q, tt=tt, kt=kt: e.transpose(self.psb[b][:, q * 128:(q + 1) * 128], ptb[:, tt, kt * 128:(kt + 1) * 128], self.identb[:]),
                          [(ptb, 0), (self.identb, 0)], [("ps", b)])
                self.copy(self.ev_eng(), pst[:, kt, t4 * 512:(t4 + 1) * 512], self.psb[b][:, 0:512], [("ps", b)], [(pst, (kt, t4))])
        P.dma("sp", self.PTT.rearrange("(kt p) t -> p kt t", p=128), pst[:], reads=[(pst, (kt, t4)) for kt in range(2) for t4 in range(T // 512)],
              writes=[P.dk_w("PTT", 0)], chan="pst")
        Th = min(T, 1024)
        for hf in range(T // Th):
            t0 = hf * Th
            self.reset()
            wg = self.ring("pwg", 2, [128, KT, 512], BF16)
            wp = self.ring("pwp", 2, [128, 2, 512], BF16)
            x2 = self.A("px2T", [128, KT, Th], BF16)
            pT = self.A("ppT", [128, 2, Th], BF16)
            sg = self.ring("psg", 2, [128, 512], F32)
            rst = self.ring("prst", 2, [128, 512], F32)
            self.load_act(x2, self.X2T, "X2T", t0, Th, KT)
            self.load_act(pT, self.PTT, "PTT", t0, Th, 2)
            state = {}

            def pre(ci, tt):
                if tt == 0:
                    slot = wp[ci % 2]
                    self.load_w(slot, self.w_ple, 0, 2, ci * 512, 512)
                    state["wp"] = slot
                slot = state["wp"]
                b = P.bank()
                for kt in range(2):
                    P.add("pe", lambda e, b=b, kt=kt, tt=tt, slot=slot: e.matmul(self.ps[b][:, 0:512], pT[:, kt, tt * 128:(tt + 1) * 128], slot[:, kt, :], start=(kt == 0), stop=(kt == 1)),
                          [(slot, kt), (pT, kt)], [("ps", b)])
                state["eb"] = b

            def epi(ci, tt, b, t0=t0):
                eb = state["eb"]
                s_ = sg[tt % 2]
                o = rst[tt % 2]
                P.add("act", lambda e: e.activation(s_[:], self.ps[b][:, 0:512], AF.Sigmoid), [("ps", b)], [(s_, 0)])
                P.add("dve", lambda e: e.tensor_tensor(o[:], self.ps[eb][:, 0:512], s_[:], ALU.mult), [("ps", eb), (s_, 0)], [(o, 0)])
                r0 = t0 + tt * 128
                P.dma("sp", self.RAW[r0:r0 + 128, ci * 512:(ci + 1) * 512], o[:], reads=[(o, 0)], writes=[P.dk_w("RAW", (r0, ci))], chan=o.name + "s")
            self.gemm_A(x2, KT, Th, self.w_pg, [i * 512 for i in range(8)], wg, 512, epi, pre=pre)

    def build(self, upto=99):
        P, T, TH = self.P, self.T, self.TH
        top = 229376 - 64
        self.ones_t = P.sb("ones_t", [128, T], F32, top - 4 * T)
        P.add("pool", lambda e: e.memset(self.ones_t[:], 1.0), [], [(self.ones_t, 0)])
        self.S_run = P.sb("S_run", [128, NH, 128], F32, top - 4 * T - 8192)
        self.S_fin = P.sb("S_fin", [128, NH, 128], F32, top - 4 * T - 16384)
        P.add("pool", lambda e: e.memset(self.S_run[:], 0.0), [], [(self.S_run, h) for h in range(NH)])
        P.add("pool", lambda e: e.memset(self.S_fin[:], 0.0), [], [(self.S_fin, h) for h in range(NH)])
        self.stage_base = self.off
        self.stage_rows("A", TH, self.xm, None, w_pre=self.nv["n_mix_pre"], dstT=self.HT, dstT_key="HT")
        if upto >= 1:
            self.stage_inproj()
        if upto >= 2:
            self.stage_hgrn_ext()
        if upto >= 3:
            self.stage_hgrn_main()
        if upto >= 4:
            self.stage_attn()
        if upto >= 5:
            self.stage_merge()
            self.stage_rows("N1", T, self.xm[128:128 + T], None, raw_src=self.RAW, raw_key="RAW", w_post=self.nv["n_mix_post"],
                            dst_tm=self.X1, dst_tm_key="X1", w_pre=self.nv["n_mlp_pre"], dstT=self.H2T, dstT_key="H2T")
        if upto >= 6:
            self.stage_mlp()
            self.stage_rows("N2", T, self.X1, "X1", raw_src=self.RAW, raw_key="RAW", w_post=self.nv["n_mlp_post"],
                            dst_tm=self.X2, dst_tm_key="X2", w_pre=None, dstT=self.X2T, dstT_key="X2T")
        if upto >= 7:
            self.stage_ple()
            self.stage_rows("N3", T, self.X2, "X2", raw_src=self.RAW, raw_key="RAW", w_post=self.nv["n_ple"],
                            dst_tm=self.out, dst_tm_key="out")
        P.emit()
        return self.nc


def _host_consts(T, S, j):
    TH = T + 256
    pos = (np.arange(TH) + j * T - 128).astype(np.float32)
    inv = (np.float32(10000.0) ** (-np.arange(0, 128, 2, dtype=np.float32) / np.float32(128))).astype(np.float32)
    ang = (pos[:, None] * inv[None, :]).astype(np.float32)
    cos = np.cos(ang).astype(np.float32).T
    sin = np.sin(ang).astype(np.float32).T
    cosT = np.concatenate([cos, cos], 0)
    sinT = np.concatenate([-sin, sin], 0)
    r = np.arange(128)[:, None]
    cc = np.arange(384)[None, :]
    band = np.abs(cc - 128 - r) <= 128
    am = np.zeros((3, 128, 384), np.float32)
    for mi in range(3):
        ok = band.copy()
        if mi == 0 and j == 0:
            ok[:, 0:128] = False
        if mi == 2 and j == 3:
            ok[:, 256:384] = False
        am[mi] = np.where(ok, 0.0, NEG)
    ident = np.eye(128, dtype=np.float32)
    s_ = np.arange(128)[:, None]
    c_ = np.arange(128)[None, :]
    tri_f = (s_ <= c_).astype(np.float32)
    tri_b = (s_ >= c_).astype(np.float32)
    cst = np.concatenate([ident, tri_f, tri_b], 1)
    perm = np.zeros((128, 128), np.float32)
    for fp in range(128):
        perm[(fp + 64) % 128, fp] = 1.0
    smask = np.ones((128, T), np.float32)
    smask[:, ::128] = 0.0
    return dict(cosT=np.ascontiguousarray(cosT), sinT=np.ascontiguousarray(sinT), amask=am, cst_f=cst, perm=perm, smask=smask)


def make_in_maps(inputs, T):
    x = np.asarray(inputs["x"])
    B, S, _ = x.shape
    assert S == 4 * T
    w_in = np.ascontiguousarray(inputs["w_in"][0])
    f_f = np.ascontiguousarray(w_in[:, C_FF:C_FF + HW])
    f_b = np.ascontiguousarray(w_in[:, C_FB:C_FB + HW])
    shared = dict(
        w_in=w_in, w_hp=np.ascontiguousarray(inputs["w_hgrn_proj"][0]), w_ap=np.ascontiguousarray(inputs["w_attn_proj"][0]),
        w_out=np.ascontiguousarray(inputs["w_out"][0]), w_up=np.ascontiguousarray(inputs["w_mlp_up"][0]),
        w_dn=np.ascontiguousarray(inputs["w_mlp_down"][0]), w_ple=np.ascontiguousarray(inputs["w_ple"][0]),
        w_pg=np.ascontiguousarray(inputs["w_ple_gate"][0]),
        n_mix_pre=np.ascontiguousarray(inputs["norm_mix_pre"][0]), n_mix_post=np.ascontiguousarray(inputs["norm_mix_post"][0]),
        n_mlp_pre=np.ascontiguousarray(inputs["norm_mlp_pre"][0]), n_mlp_post=np.ascontiguousarray(inputs["norm_mlp_post"][0]),
        n_ple=np.ascontiguousarray(inputs["norm_ple"][0]), hgn=np.ascontiguousarray(inputs["hgrn_norm"][0]),
        sink=np.ascontiguousarray(inputs["attn_sink"][0]),
    )
    lbf = np.asarray(inputs["lb_fwd"]).reshape(2, NH, 128).transpose(0, 2, 1)
    lbb = np.asarray(inputs["lb_bwd"]).reshape(2, NH, 128).transpose(0, 2, 1)
    lbm = np.ascontiguousarray(np.stack([lbf, lbb], 0)).astype(np.float32)
    maps = []
    for c in range(8):
        b, j = c // 4, c % 4
        xm = np.zeros((T + 256, D), np.float32)
        lo, hi = j * T - 128, (j + 1) * T + 128
        slo, shi = max(lo, 0), min(hi, S)
        xm[slo - lo:shi - lo] = x[b, slo:shi]
        xe = np.empty((3, T, D), np.float32)
        wfx = np.empty((3, D, HW), np.float32)
        lbx = np.empty((3, 2, 128, NH), np.float32)
        flags = np.zeros(6, np.float32)
        for e in range(3):
            if e < j:
                xe[e] = x[b, e * T:(e + 1) * T]
                wfx[e] = f_f
                lbx[e] = lbf
            else:
                ch = 3 + j - e
                xe[e] = x[b, ch * T:(ch + 1) * T][::-1]
                wfx[e] = f_b
                lbx[e] = lbb
            flags[2 * e] = 1.0 if e == j - 1 else 0.0
            flags[2 * e + 1] = 0.0 if e == j - 1 else 1.0
        m = dict(shared)
        m.update(xm=xm, xe=xe, p=np.ascontiguousarray(inputs["p"][0, b, j * T:(j + 1) * T]), wfx=wfx, lbx=lbx, lbm=lbm, flags=flags)
        m.update(_host_consts(T, S, j))
        maps.append(m)
    return maps


def kernel(**inputs):
    x = np.asarray(inputs["x"])
    B, S, _ = x.shape
    T = S // 4
    nc = Builder(T).build()
    maps = make_in_maps(inputs, T)
    res = run_bass_kernel_spmd(nc, maps, core_ids=list(range(8)))
    out = np.empty((B, S, D), np.float32)
    for c in range(8):
        b, j = c // 4, c % 4
        out[b, j * T:(j + 1) * T] = res.results[c]["out"]
    return out
```
